# Optimizing a Trainium2 kernel written in Bass

```python
import jax
import jax.numpy as jnp
from jax import lax
import numpy as np

D_MODEL = 1024
BATCH = 2
SEQ = 8192
DEPTH = 1

GRID_W = 64
CTX_LEN = 256
HEAD_DIM = 64
NA_HEADS = 8
RW_HEADS = 8
NA_WIDTH = NA_HEADS * HEAD_DIM
RW_WIDTH = RW_HEADS * HEAD_DIM
MIX_WIDTH = NA_WIDTH + RW_WIDTH
WIN_H = 8
WIN_W = 16
ROPE_BASE = 10000.0
N_DIR = 2
DECAY_LORA = 64
AAA_LORA = 64
GATE_LORA = 128
NA_COLS = 3 * NA_WIDTH
RW_COLS = 3 * RW_WIDTH + N_DIR * DECAY_LORA + N_DIR * AAA_LORA + GATE_LORA
IN_COLS = NA_COLS + RW_COLS
D_FF = 4 * D_MODEL
N_MOD = 6
LN_EPS = 1e-6
GN_EPS = 64e-5
ALPHA = (2 * DEPTH) ** 0.25
BETA = (8 * DEPTH) ** -0.25

kernel_name = 'hybrid_natten_rwkv7_dit_layer'


def _normalize(x, eps):
    xf = x.astype(jnp.float32)
    mu = jnp.mean(xf, axis=-1, keepdims=True)
    var = jnp.mean(jnp.square(xf - mu), axis=-1, keepdims=True)
    return (xf - mu) * lax.rsqrt(var + eps)


def layer_norm(x, gain, bias):
    return (_normalize(x, LN_EPS) * gain + bias).astype(x.dtype)


def modulate(x, shift, scale):
    return (_normalize(x, LN_EPS) * (1 + scale) + shift).astype(x.dtype)


def split_heads(z, n_heads):
    return z.reshape(z.shape[:-1] + (n_heads, HEAD_DIM))


def token_shift(p, mu_prev, mu_next):
    prev = jnp.pad(p, ((0, 0), (1, 0), (0, 0)))[:, :-1]
    nxt = jnp.pad(p, ((0, 0), (0, 1), (0, 0)))[:, 1:]
    return p + mu_prev * (prev - p) + mu_next * (nxt - p)


def rope_axis(x, pos):
    f = x.shape[-1] // 2
    inv = ROPE_BASE ** (-jnp.arange(f, dtype=jnp.float32) / f)
    ang = pos[:, None] * inv[None, :]
    cos = jnp.cos(ang)[:, None, :]
    sin = jnp.sin(ang)[:, None, :]
    x1, x2 = x[..., :f], x[..., f:]
    return jnp.concatenate([x1 * cos - x2 * sin, x1 * sin + x2 * cos], axis=-1).astype(x.dtype)


def axial_rope(x, row, col):
    h = x.shape[-1] // 2
    return jnp.concatenate([rope_axis(x[..., :h], row), rope_axis(x[..., h:], col)], axis=-1)


def neighbourhood_attention(q, k, v, kc, vc, rpb):
    B, T, H, Dh = q.shape
    rows = T // GRID_W
    kh = min(WIN_H, rows)
    kw = WIN_W
    t = jnp.arange(T)
    row = (t // GRID_W).astype(jnp.float32)
    col = (t % GRID_W).astype(jnp.float32)
    q_rot = axial_rope(q, row, col)
    k_rot = axial_rope(k, row, col)

    def to_grid(a):
        return a.reshape(B, rows, GRID_W, H, Dh).transpose(0, 3, 1, 2, 4)

    k_grid = to_grid(k_rot)
    v_grid = to_grid(v)
    q_rows = to_grid(q_rot).transpose(2, 0, 1, 3, 4)
    qp_rows = to_grid(q).transpose(2, 0, 1, 3, 4)
    kc_h = kc.transpose(0, 2, 1, 3)
    vc_h = vc.transpose(0, 2, 1, 3)
    cols = jnp.arange(GRID_W)
    col_idx = jnp.clip(cols - kw // 2, 0, GRID_W - kw)[:, None] + jnp.arange(kw)[None, :]
    col_off = col_idx - cols[:, None] + (WIN_W - 1)
    scale = Dh ** -0.5

    def one_row(args):
        i, q_i, qp_i = args
        r0 = jnp.clip(i - kh // 2, 0, rows - kh)
        row_off = r0 + jnp.arange(kh) - i + (WIN_H - 1)
        bias = rpb[:, row_off][:, :, col_off].transpose(0, 2, 1, 3)
        k_nb = lax.dynamic_slice_in_dim(k_grid, r0, kh, axis=2)[:, :, :, col_idx]
        v_nb = lax.dynamic_slice_in_dim(v_grid, r0, kh, axis=2)[:, :, :, col_idx]
        s_loc = jnp.einsum('bhqd,bhrqkd->bhqrk', q_i, k_nb) * scale + bias
        s_ctx = jnp.einsum('bhqd,bhld->bhql', qp_i, kc_h) * scale
        s = jnp.concatenate([s_loc.reshape(B, H, GRID_W, kh * kw), s_ctx], axis=-1).astype(jnp.float32)
        p = jax.nn.softmax(s, axis=-1).astype(v.dtype)
        p_loc = p[..., :kh * kw].reshape(B, H, GRID_W, kh, kw)
        p_ctx = p[..., kh * kw:]
        return (jnp.einsum('bhqrk,bhrqkd->bhqd', p_loc, v_nb)
                + jnp.einsum('bhql,bhld->bhqd', p_ctx, vc_h))

    out = lax.map(one_row, (jnp.arange(rows), q_rows, qp_rows))
    return out.transpose(1, 0, 3, 2, 4).reshape(B, T, H * Dh)


def context_attention(qc, kc, vc):
    B, L, H, Dh = qc.shape
    s = (jnp.einsum('bqhd,bkhd->bhqk', qc, kc) * Dh ** -0.5).astype(jnp.float32)
    p = jax.nn.softmax(s, axis=-1).astype(vc.dtype)
    return jnp.einsum('bhqk,bkhd->bqhd', p, vc).reshape(B, L, H * Dh)


def rwkv_terms(p, w0, w2, a0, a2, g2, k_k, k_a):
    B, T, _ = p.shape
    cuts = [RW_WIDTH, 2 * RW_WIDTH, 3 * RW_WIDTH,
            3 * RW_WIDTH + N_DIR * DECAY_LORA,
            3 * RW_WIDTH + N_DIR * (DECAY_LORA + AAA_LORA)]
    r, k, v, pw, pa, pg = jnp.split(p, cuts, axis=-1)
    pw = pw.reshape(B, T, N_DIR, DECAY_LORA)
    pa = pa.reshape(B, T, N_DIR, AAA_LORA)
    w = -jax.nn.softplus(-(w0 + jnp.einsum('btdr,drc->btdc', jnp.tanh(pw), w2))) - 0.5
    decay = jnp.exp(-jnp.exp(w))
    a = jax.nn.sigmoid(a0 + jnp.einsum('btdr,drc->btdc', pa, a2))
    g = jax.nn.sigmoid(pg) @ g2
    kk = split_heads(k * k_k, RW_HEADS)
    kk = kk / jnp.maximum(jnp.sqrt(jnp.sum(jnp.square(kk), axis=-1, keepdims=True)), 1e-12)
    kd = k[:, :, None] * (1 + (a - 1) * k_a)
    return (split_heads(r, RW_HEADS), split_heads(decay, RW_HEADS), split_heads(kd, RW_HEADS),
            split_heads(v, RW_HEADS), kk, split_heads(a, RW_HEADS), g)


def rwkv_scan(s0, terms, d, reverse):
    r, decay, kd, v, kk, a, _ = terms
    xs = tuple(z.astype(jnp.float32).transpose(1, 0, 2, 3)
               for z in (r, decay[:, :, d], kd[:, :, d], v, kk, a[:, :, d]))

    def step(S, inp):
        r_t, w_t, k_t, v_t, kk_t, a_t = inp
        s_kk = jnp.einsum('bhij,bhj->bhi', S, kk_t)
        S = (S * w_t[:, :, None, :] - s_kk[..., None] * (kk_t * a_t)[:, :, None, :]
             + v_t[..., None] * k_t[:, :, None, :])
        return S, jnp.einsum('bhij,bhj->bhi', S, r_t)

    S, y = lax.scan(step, s0, xs, reverse=reverse)
    return S, y.transpose(1, 0, 2, 3)


def rwkv_readout(y, terms, r_k, gn_g, gn_b):
    r, _, kd, v, _, _, g = terms
    B, T = r.shape[:2]
    yn = _normalize(y, GN_EPS).reshape(B, T, RW_WIDTH) * gn_g + gn_b
    bonus = jnp.sum(jnp.sum(r[:, :, None] * kd * r_k, axis=-1, keepdims=True) * v[:, :, None], axis=2)
    return ((yn + bonus.reshape(B, T, RW_WIDTH)) * g).astype(v.dtype)


def rwkv_mixer(p, pc, w0, w2, a0, a2, g2, k_k, k_a, r_k, gn_g, gn_b, update_ctx):
    lat = rwkv_terms(p, w0, w2, a0, a2, g2, k_k, k_a)
    ctx = rwkv_terms(pc, w0, w2, a0, a2, g2, k_k, k_a)
    B = p.shape[0]
    y_lat, y_ctx = [], []
    for d in range(N_DIR):
        s0 = jnp.zeros((B, RW_HEADS, HEAD_DIM, HEAD_DIM), jnp.float32)
        s_ctx, yc = rwkv_scan(s0, ctx, d, d == 1)
        _, yl = rwkv_scan(s_ctx, lat, d, d == 1)
        y_lat.append(yl)
        y_ctx.append(yc)
    out = rwkv_readout(y_lat[0] + y_lat[1], lat, r_k, gn_g, gn_b)
    out_c = rwkv_readout(y_ctx[0] + y_ctx[1], ctx, r_k, gn_g, gn_b) if update_ctx else None
    return out, out_c


def squared_relu_mlp(h, w1, w2):
    return jnp.square(jax.nn.relu(h @ w1)) @ w2


def trunk_layer(x, xc, c, c_ctx, w_mod, b_mod, w_in, w_out, ln1_g, ln1_b, mlp_w1, mlp_w2,
                ln2_g, ln2_b, na_rpb, rw_mu_prev, rw_mu_next, rw_w0, rw_w2, rw_a0, rw_a2, rw_g2,
                rw_k_k, rw_k_a, rw_r_k, rw_gn_g, rw_gn_b, update_ctx):
    mod = jax.nn.silu(c) @ w_mod + b_mod
    mod_c = jax.nn.silu(c_ctx) @ w_mod + b_mod
    sh1, sc1, g1, sh2, sc2, g2 = jnp.split(mod[:, None, :], N_MOD, axis=-1)
    sh1c, sc1c, g1c, sh2c, sc2c, g2c = jnp.split(mod_c, N_MOD, axis=-1)

    p = modulate(x, sh1, sc1) @ w_in
    pc = modulate(xc, sh1c, sc1c) @ w_in
    q, k, v = [split_heads(z, NA_HEADS) for z in jnp.split(p[..., :NA_COLS], 3, axis=-1)]
    qc, kc, vc = [split_heads(z, NA_HEADS) for z in jnp.split(pc[..., :NA_COLS], 3, axis=-1)]
    y_na = neighbourhood_attention(q, k, v, kc, vc, na_rpb)
    p_rw = token_shift(p[..., NA_COLS:], rw_mu_prev, rw_mu_next)
    pc_rw = token_shift(pc[..., NA_COLS:], rw_mu_prev, rw_mu_next)
    y_rw, yc_rw = rwkv_mixer(p_rw, pc_rw, rw_w0, rw_w2, rw_a0, rw_a2, rw_g2, rw_k_k, rw_k_a,
                             rw_r_k, rw_gn_g, rw_gn_b, update_ctx)

    y = jnp.concatenate([y_na, y_rw], axis=-1) @ w_out
    x = layer_norm(ALPHA * x + g1 * y, ln1_g, ln1_b)
    x = layer_norm(ALPHA * x + g2 * squared_relu_mlp(modulate(x, sh2, sc2), mlp_w1, mlp_w2), ln2_g, ln2_b)
    if update_ctx:
        yc = jnp.concatenate([context_attention(qc, kc, vc), yc_rw], axis=-1) @ w_out
        xc = layer_norm(ALPHA * xc + g1c * yc, ln1_g, ln1_b)
        xc = layer_norm(ALPHA * xc + g2c * squared_relu_mlp(modulate(xc, sh2c, sc2c), mlp_w1, mlp_w2),
                        ln2_g, ln2_b)
    return x, xc


def setup_inputs(seed: int = 0) -> dict:
    key = jax.random.key(seed)
    ks = jax.random.split(key, 27)
    L = DEPTH

    def nrm(k, shape, s):
        return jax.random.normal(k, shape, jnp.float32) * s

    def uni(k, shape, lo, hi):
        return jax.random.uniform(k, shape, jnp.float32, lo, hi)

    return {
        'x': nrm(ks[0], (BATCH, SEQ, D_MODEL), 1.0),
        'c': nrm(ks[1], (BATCH, D_MODEL), 1.0),
        'ctx': nrm(ks[2], (BATCH, CTX_LEN, D_MODEL), 1.0),
        'c_ctx': nrm(ks[3], (D_MODEL,), 1.0),
        'w_mod': nrm(ks[4], (L, D_MODEL, N_MOD * D_MODEL), 0.5 * D_MODEL ** -0.5),
        'b_mod': nrm(ks[5], (L, N_MOD * D_MODEL), 0.01),
        'w_in': nrm(ks[6], (L, D_MODEL, IN_COLS), D_MODEL ** -0.5),
        'w_out': nrm(ks[7], (L, MIX_WIDTH, D_MODEL), BETA * MIX_WIDTH ** -0.5),
        'ln1_g': 1.0 + nrm(ks[8], (L, D_MODEL), 0.01),
        'ln1_b': nrm(ks[9], (L, D_MODEL), 0.01),
        'mlp_w1': nrm(ks[10], (L, D_MODEL, D_FF), D_MODEL ** -0.5),
        'mlp_w2': nrm(ks[11], (L, D_FF, D_MODEL), BETA * D_FF ** -0.5),
        'ln2_g': 1.0 + nrm(ks[12], (L, D_MODEL), 0.01),
        'ln2_b': nrm(ks[13], (L, D_MODEL), 0.01),
        'na_rpb': nrm(ks[14], (L, NA_HEADS, 2 * WIN_H - 1, 2 * WIN_W - 1), 0.1),
        'rw_mu_prev': uni(ks[15], (L, RW_COLS), 0.0, 0.5),
        'rw_mu_next': uni(ks[16], (L, RW_COLS), 0.0, 0.5),
        'rw_w0': uni(ks[17], (L, N_DIR, RW_WIDTH), -3.0, 0.0),
        'rw_w2': nrm(ks[18], (L, N_DIR, DECAY_LORA, RW_WIDTH), 0.5 * DECAY_LORA ** -0.5),
        'rw_a0': nrm(ks[19], (L, N_DIR, RW_WIDTH), 0.1),
        'rw_a2': nrm(ks[20], (L, N_DIR, AAA_LORA, RW_WIDTH), 0.5 * AAA_LORA ** -0.5),
        'rw_g2': nrm(ks[21], (L, GATE_LORA, RW_WIDTH), GATE_LORA ** -0.5),
        'rw_k_k': 0.85 + nrm(ks[22], (L, RW_WIDTH), 0.02),
        'rw_k_a': 1.0 + nrm(ks[23], (L, RW_WIDTH), 0.02),
        'rw_r_k': nrm(ks[24], (L, RW_HEADS, HEAD_DIM), 0.1),
        'rw_gn_g': 1.0 + nrm(ks[25], (L, RW_WIDTH), 0.01),
        'rw_gn_b': nrm(ks[26], (L, RW_WIDTH), 0.01),
    }


def reference(x, c, ctx, c_ctx, w_mod, b_mod, w_in, w_out, ln1_g, ln1_b, mlp_w1, mlp_w2,
              ln2_g, ln2_b, na_rpb, rw_mu_prev, rw_mu_next, rw_w0, rw_w2, rw_a0, rw_a2, rw_g2,
              rw_k_k, rw_k_a, rw_r_k, rw_gn_g, rw_gn_b):
    xc = ctx
    for l in range(DEPTH):
        x, xc = trunk_layer(x, xc, c, c_ctx, w_mod[l], b_mod[l], w_in[l], w_out[l], ln1_g[l], ln1_b[l],
                            mlp_w1[l], mlp_w2[l], ln2_g[l], ln2_b[l], na_rpb[l], rw_mu_prev[l],
                            rw_mu_next[l], rw_w0[l], rw_w2[l], rw_a0[l], rw_a2[l], rw_g2[l],
                            rw_k_k[l], rw_k_a[l], rw_r_k[l], rw_gn_g[l], rw_gn_b[l],
                            update_ctx=l < DEPTH - 1)
    return x
```

```python
import numpy as np
from contextlib import ExitStack, contextmanager
import concourse.bass as bass
import concourse.mybir as mybir
from concourse.bass_utils import run_bass_kernel_spmd

F32 = mybir.dt.float32
BF16 = mybir.dt.bfloat16
AF = mybir.ActivationFunctionType
ALU = mybir.AluOpType

D = 1024
T = 8192
L = 256
NTOK = T + L
ALPHA = 2.0 ** 0.25
NEG = -30000.0
WCOLS = 1408
PRW_ROWS = NTOK + 4


class Buf:
    def __init__(self, t, name):
        self.t = t
        self.name = name
        self.w = None
        self.r = []
        self.psum = False


class KB:
    NDMA = 6

    def __init__(self, nc):
        self.nc = nc
        self.eng = {'pe': nc.tensor, 'act': nc.scalar, 'dve': nc.vector,
                    'pool': nc.gpsimd, 'sp': nc.sync}
        self.sem = {}
        self.cnt = {}
        for e in ['pe', 'act', 'dve', 'pool']:
            self._mksem(e)
        self.dq = {}
        for q in ['sp', 'pool']:
            names = ["dma_%s_%d" % (q, i) for i in range(self.NDMA)]
            for n in names:
                self._mksem(n)
            self.dq[q] = [names, 0]
        self.seen = {e: {} for e in self.eng}
        self.out_events = []
        self.uid = 0
        self.stack = None
        import os as _os
        self.relax = _os.environ.get('KDEV_RELAX', '0') == '1'

    def _mksem(self, name):
        self.sem[name] = self.nc.alloc_semaphore(name)
        self.cnt[name] = 0

    def sb(self, name, shape, dt):
        self.uid += 1
        nm = "%s_%d" % (name, self.uid)
        return Buf(self.stack.enter_context(self.nc.sbuf_tensor(nm, list(shape), dt)), nm)

    def ps(self, name, shape, dt=F32):
        self.uid += 1
        nm = "%s_%d" % (name, self.uid)
        b = Buf(self.stack.enter_context(self.nc.psum_tensor(nm, list(shape), dt)), nm)
        b.psum = True
        return b

    def dram(self, name, shape, dt):
        return Buf(self.nc.dram_tensor(name, list(shape), dt), name)

    @contextmanager
    def phase(self):
        old = self.stack
        with ExitStack() as st:
            self.stack = st
            yield
            self.barrier()
        self.stack = old

    def barrier(self):
        for e in ['pe', 'act', 'dve', 'pool', 'sp']:
            for s, v in self.cnt.items():
                if v > 0 and self.seen[e].get(s, 0) < v:
                    self.eng[e].wait_ge(self.sem[s], v)
                    self.seen[e][s] = v

    def _need(self, engine, ev, waits):
        if ev is None:
            return
        s, v = ev
        if engine == 'pe' and s == 'pe':
            return
        if self.relax and engine == s and engine in ('act', 'dve'):
            return
        if self.seen[engine].get(s, 0) >= v:
            return
        if waits.get(s, 0) < v:
            waits[s] = v

    def _emit_waits(self, engine, waits):
        e = self.eng[engine]
        for s, v in waits.items():
            e.wait_ge(self.sem[s], v)
            self.seen[engine][s] = v

    def _record(self, ev, R, W):
        for b in R:
            b.r.append(ev)
            if len(b.r) > 16:
                best = {}
                for s, v in b.r:
                    if best.get(s, 0) < v:
                        best[s] = v
                b.r = list(best.items())
        for b in W:
            b.w = ev
            b.r = []

    def op(self, engine, fn, R=(), W=(), acc=False):
        W = list(W) + [b for b in R if b.psum]
        R = [b for b in R if not b.psum]
        waits = {}
        for b in R:
            self._need(engine, b.w, waits)
        for b in W:
            if not acc:
                self._need(engine, b.w, waits)
                for ev in b.r:
                    self._need(engine, ev, waits)
        self._emit_waits(engine, waits)
        ins = fn(self.eng[engine])
        self.cnt[engine] += 1
        ins.then_inc(self.sem[engine], 1)
        self._record((engine, self.cnt[engine]), R, W)

    def dma(self, q, out_ap, in_ap, R=(), W=(), out=False, **kw):
        names, idx = self.dq[q]
        s = names[idx % len(names)]
        self.dq[q][1] = idx + 1
        waits = {}
        if self.cnt[s] > 0:
            self._need(q, (s, self.cnt[s]), waits)
        for b in R:
            self._need(q, b.w, waits)
        for b in W:
            self._need(q, b.w, waits)
            for ev in b.r:
                self._need(q, ev, waits)
        self._emit_waits(q, waits)
        ins = self.eng[q].dma_start(out=out_ap, in_=in_ap, **kw)
        self.cnt[s] += 16
        ins.then_inc(self.sem[s], 16)
        ev = (s, self.cnt[s])
        self._record(ev, R, W)
        if out:
            self.out_events.append(ev)
        return ev

    def finish(self):
        self.barrier()


def ln_stats(kb, xt, xap, st, mv, rstd, nmr, eps):
    kb.op('dve', lambda e: e.bn_stats(st.t[:, 0, :], xap[:, 0:512]), R=[xt], W=[st])
    kb.op('dve', lambda e: e.bn_stats(st.t[:, 1, :], xap[:, 512:1024]), R=[xt], W=[st])
    kb.op('dve', lambda e: e.bn_aggr(mv.t[:], st.t[:]), R=[st], W=[mv])
    kb.op('act', lambda e: e.activation(rstd.t[:, 0:1], mv.t[:, 1:2], AF.Sqrt, bias=eps, scale=1.0), R=[mv], W=[rstd])
    kb.op('dve', lambda e: e.reciprocal(rstd.t[:, 0:1], rstd.t[:, 0:1]), R=[rstd], W=[rstd])
    kb.op('dve', lambda e: e.scalar_tensor_tensor(nmr.t[:, 0:1], mv.t[:, 0:1], -1.0, rstd.t[:, 0:1], ALU.mult, ALU.mult),
          R=[mv, rstd], W=[nmr])


def build_program(stages=("mod", "inproj", "na", "rwkv", "oproj", "gather", "b1"), debug=False):
    nc = bass.Bass("TRN2", target_bir_lowering=False)

    def din(name, shape, dt=F32):
        return nc.dram_tensor(name, list(shape), dt, kind="ExternalInput").ap()

    A = {}
    A['xa'] = din("xa", [NTOK, D])
    A['cct'] = din("cct", [D, 2])
    A['wmod'] = din("wmod", [D, 6 * D])
    A['bmod'] = din("bmod", [6 * D, 2])
    A['wina'] = din("wina", [D, WCOLS])
    A['cosd'] = din("cosd", [128, T])
    A['sind'] = din("sind", [128, T])
    A['biasg'] = din("biasg", [128, 5 * 2 * 640])
    A['maskc'] = din("maskc", [128, 5 * 2 * 640])
    A['wo'] = din("wo", [256, D])
    A['xb'] = din("xb", [2048, D])
    A['w1'] = din("w1", [D, 4 * D])
    A['w2'] = din("w2", [4 * D, D])
    A['lnv'] = din("lnv", [4, D])
    A['rwv'] = din("rwv", [16, 768])
    A['rww2'] = din("rww2", [128, 256])
    A['rwa2'] = din("rwa2", [128, 256])
    A['rwg2'] = din("rwg2", [128, 128])
    A['tri'] = din("tri", [128, 6 * 128])
    out = nc.dram_tensor("out", [2048, D], F32, kind="ExternalOutput").ap()
    dbg = {}
    if debug:
        dbg['ycat'] = nc.dram_tensor("dbg_ycat", [T, 256], BF16, kind="ExternalOutput").ap()
        dbg['modT'] = nc.dram_tensor("dbg_modT", [128, 96], F32, kind="ExternalOutput").ap()
        dbg['prw'] = nc.dram_tensor("dbg_prw", [PRW_ROWS, 768], F32, kind="ExternalOutput").ap()
        dbg['x1'] = nc.dram_tensor("dbg_x1", [2048, D], F32, kind="ExternalOutput").ap()
        dbg['yrw_in'] = din("dbg_yrw", [T, 128], BF16)

    kb = KB(nc)
    prw = kb.dram("prw", [PRW_ROWS, 768], F32)
    ycat = kb.dram("ycat", [T, 256], BF16)
    opart = kb.dram("opart", [T, D], F32)
    osum = kb.dram("osum", [2048, D], F32)
    x1d = kb.dram("x1d", [2048, D], F32)

    with ExitStack() as outer:
        kb.stack = outer
        modT = kb.sb("modT", [128, 48, 2], F32)
        modC = kb.sb("modC", [128, 48, 2], F32)
        ident = kb.sb("ident", [128, 128], BF16)
        identf = kb.sb("identf", [128, 128], F32)
        onesf = kb.sb("onesf", [128, 128], F32)
        kb.op('pool', lambda e: e.memset(identf.t[:], 0.0), W=[identf])
        kb.op('pool', lambda e: e.affine_select(identf.t[:], identf.t[:], pattern=[[-1, 128]],
                                                compare_op=ALU.not_equal, fill=1.0, base=0,
                                                channel_multiplier=1), R=[identf], W=[identf])
        kb.op('dve', lambda e: e.tensor_copy(ident.t[:], identf.t[:]), R=[identf], W=[ident])
        kb.op('pool', lambda e: e.memset(onesf.t[:], 1.0), W=[onesf])

        if "mod" in stages:
            with kb.phase():
                cT = kb.sb("cT", [128, 8, 2], F32)
                scT = kb.sb("scT", [128, 8, 2], F32)
                bT = kb.sb("bT", [128, 48, 2], F32)
                kb.dma('sp', cT.t[:], A['cct'].rearrange("(k p) r -> p k r", p=128), W=[cT])
                kb.dma('sp', bT.t[:], A['bmod'].rearrange("(j p) r -> p j r", p=128), W=[bT])
                kb.op('act', lambda e: e.activation(scT.t[:], cT.t[:], AF.Silu), R=[cT], W=[scT])
                wm = [kb.sb("wm%d" % i, [128, 8, 512], F32) for i in range(2)]
                pm = [kb.ps("pm%d" % i, [128, 512], F32) for i in range(2)]
                for jg in range(12):
                    w = wm[jg % 2]
                    ps = pm[jg % 2]
                    kb.dma('sp' if jg % 2 == 0 else 'pool', w.t[:],
                           A['wmod'][:, jg * 512:(jg + 1) * 512].rearrange("(k p) n -> p k n", p=128), W=[w])
                    for jj in range(4):
                        for k in range(8):
                            kb.op('pe', lambda e: e.matmul(ps.t[:, jj * 2:(jj + 1) * 2],
                                                           w.t[:, k, jj * 128:(jj + 1) * 128],
                                                           scT.t[:, k, :], start=(k == 0), stop=(k == 7)),
                                  R=[w, scT], W=[ps], acc=(k > 0))
                    kb.op('dve', lambda e: e.tensor_tensor(
                        modT.t[:, jg * 4:(jg + 1) * 4, :],
                        ps.t[:, 0:8].rearrange("p (j r) -> p j r", r=2),
                        bT.t[:, jg * 4:(jg + 1) * 4, :], ALU.add), R=[ps, bT], W=[modT])
                kb.op('dve', lambda e: e.tensor_scalar_add(modT.t[:, 8:16, :], modT.t[:, 8:16, :], 1.0), R=[modT], W=[modT])
                kb.op('dve', lambda e: e.tensor_scalar_add(modT.t[:, 32:40, :], modT.t[:, 32:40, :], 1.0), R=[modT], W=[modT])
                kb.op('dve', lambda e: e.tensor_copy(modC.t[:], modT.t[:]), R=[modT], W=[modC])
                kb.op('dve', lambda e: e.tensor_copy(modC.t[:, :, 0:1], modT.t[:, :, 1:2]), R=[modT], W=[modC])
                if debug:
                    kb.dma('sp', dbg['modT'], modT.t[:].rearrange("p j r -> p (j r)"), R=[modT], out=True)

        if "inproj" in stages:
            stA = ExitStack()
            outerA = kb.stack
            kb.stack = stA
            QrT = kb.sb("QrT", [128, T], BF16)
            QpT = kb.sb("QpT", [128, T], BF16)
            KrT = kb.sb("KrT", [128, T], BF16)
            KcT = kb.sb("KcT", [128, L], BF16)
            Vaug = kb.sb("Vaug", [128, 66, 2, 80], BF16)
            kb.op('pool', lambda e: e.memset(Vaug.t[:], 1.0), W=[Vaug])
            with kb.phase():
                wb = kb.sb("winb", [128, 8, WCOLS], BF16)
                wst = [kb.sb("wst%d" % i, [128, 2, WCOLS], F32) for i in range(2)]
                for kk in range(4):
                    w = wst[kk % 2]
                    kb.dma('pool', w.t[:], A['wina'][kk * 256:(kk + 1) * 256, :].rearrange("(k p) n -> p k n", p=128), W=[w])
                    kb.op('pool', lambda e: e.tensor_copy(wb.t[:, kk * 2:(kk + 1) * 2, :], w.t[:]), R=[w], W=[wb])
                zrow = kb.sb("zrow", [1, 768], F32)
                kb.op('pool', lambda e: e.memset(zrow.t[:], 0.0), W=[zrow])
                for rr in (0, 257, 258, PRW_ROWS - 1):
                    kb.dma('pool', prw.t.ap()[rr:rr + 1, :], zrow.t[:], R=[zrow], W=[prw])
                xt = [kb.sb("xt%d" % i, [128, D], F32) for i in range(2)]
                xn = [kb.sb("xn%d" % i, [128, D], BF16) for i in range(2)]
                st_ = [kb.sb("st%d" % i, [128, 2, 6], F32) for i in range(2)]
                mv_ = [kb.sb("mv%d" % i, [128, 2], F32) for i in range(2)]
                rstd_ = [kb.sb("rstd%d" % i, [128, 2], F32) for i in range(2)]
                nmr_ = [kb.sb("nmr%d" % i, [128, 2], F32) for i in range(2)]
                hT = [kb.sb("hT%d" % i, [128, 8, 512], BF16) for i in range(2)]
                ptr = [kb.ps("ptr%d" % i, [128, 1024], BF16) for i in range(2)]
                pf = [kb.ps("pf%d" % i, [128, 512], F32) for i in range(4)]
                pt_ = [kb.ps("ptk%d" % i, [128, 512], F32) for i in range(2)]
                cs = kb.sb("cs", [128, 512], F32)
                sn = kb.sb("sn", [128, 512], F32)
                t1 = [kb.sb("t1_%d" % i, [128, 512], F32) for i in range(2)]
                t2 = [kb.sb("t2_%d" % i, [128, 512], F32) for i in range(2)]
                rwsb = [kb.sb("rwsb%d" % i, [128, 768], F32) for i in range(2)]
                nsubtot = 0
                import os as _os
                for bi in range(int(_os.environ.get('KDEV_NBLK', '17'))):
                    isctx = (bi == 0)
                    nsub = 2 if isctx else 4
                    g0 = 0 if isctx else L + (bi - 1) * 512
                    ntok = nsub * 128
                    mm = modC if isctx else modT
                    h = hT[bi % 2]
                    for sub in range(nsub):
                        xx = xt[nsubtot % 2]
                        xb = xn[nsubtot % 2]
                        pp = ptr[nsubtot % 2]
                        kb.dma('sp', xx.t[:], A['xa'][g0 + sub * 128: g0 + (sub + 1) * 128, :], W=[xx])
                        st, mv, rstd, nmr = (z[nsubtot % 2] for z in (st_, mv_, rstd_, nmr_))
                        ln_stats(kb, xx, xx.t, st, mv, rstd, nmr, 1e-6)
                        kb.op('act', lambda e: e.activation(xb.t[:], xx.t[:], AF.Identity, bias=nmr.t[:, 0:1], scale=rstd.t[:, 0:1]),
                              R=[xx, nmr, rstd], W=[xb])
                        _pp = float(_os.environ.get('KDEV_PART', '9'))
                        for k in range(8 if _pp >= 0.5 else 0):
                            kb.op('pe', lambda e: e.transpose(pp.t[:, k * 128:(k + 1) * 128], xb.t[:, k * 128:(k + 1) * 128], ident.t[:]),
                                  R=[xb, ident], W=[pp], acc=(k > 0))
                        for k in range(8 if _pp >= 0.7 else 0):
                            kb.op('act', lambda e: e.activation(h.t[:, k, sub * 128:(sub + 1) * 128], pp.t[:, k * 128:(k + 1) * 128],
                                                                AF.Identity, bias=mm.t[:, k, 0:1], scale=mm.t[:, 8 + k, 0:1]),
                                  R=[pp, mm], W=[h])
                        nsubtot += 1
                    if float(_os.environ.get('KDEV_PART', '9')) < 2:
                        continue
                    if isctx:
                        p = pf[2]
                        for k in range(8):
                            kb.op('pe', lambda e: e.matmul(p.t[:, 0:ntok], wb.t[:, k, 256:384], h.t[:, k, 0:ntok], start=(k == 0), stop=(k == 7)),
                                  R=[wb, h], W=[p], acc=(k > 0))
                        kb.op('act', lambda e: e.copy(KcT.t[:, 0:ntok], p.t[:, 0:ntok]), R=[p], W=[KcT])
                    else:
                        l0 = g0 - L
                        kb.dma('sp', cs.t[:], A['cosd'][:, l0:l0 + 512], W=[cs])
                        kb.dma('sp', sn.t[:], A['sind'][:, l0:l0 + 512], W=[sn])
                        for og in range(4):
                            p = pf[og]
                            for k in range(8):
                                kb.op('pe', lambda e: e.matmul(p.t[:], wb.t[:, k, og * 128:(og + 1) * 128], h.t[:, k, :], start=(k == 0), stop=(k == 7)),
                                      R=[wb, h], W=[p], acc=(k > 0))
                        _sk = _os.environ.get('KDEV_SKIP', '')
                        if 'q' not in _sk:
                            kb.op('act', lambda e: e.copy(QpT.t[:, l0:l0 + 512], pf[0].t[:]), R=[pf[0]], W=[QpT])
                        for qi, dst in (((0, QrT), (1, KrT)) if 'r' not in _sk else ()):
                            a1, a2 = t1[qi], t2[qi]
                            kb.op('dve', lambda e: e.tensor_tensor(a1.t[:], pf[2 * qi].t[:], cs.t[:], ALU.mult), R=[pf[2 * qi], cs], W=[a1])
                            kb.op('dve', lambda e: e.tensor_tensor(a2.t[:], pf[2 * qi + 1].t[:], sn.t[:], ALU.mult), R=[pf[2 * qi + 1], sn], W=[a2])
                            if 'p' not in _sk:
                                kb.op('pool', lambda e: e.tensor_tensor(dst.t[:, l0:l0 + 512], a1.t[:], a2.t[:], ALU.add), R=[a1, a2], W=[dst])
                    if float(_os.environ.get('KDEV_PART', '9')) < 3:
                        continue
                    for sub in range(nsub):
                        pa, pb = pt_[0], pt_[1]
                        for k in range(8):
                            kb.op('pe', lambda e: e.matmul(pa.t[:], h.t[:, k, sub * 128:(sub + 1) * 128], wb.t[:, k, 512:1024], start=(k == 0), stop=(k == 7)),
                                  R=[wb, h], W=[pa], acc=(k > 0))
                        for k in range(8):
                            kb.op('pe', lambda e: e.matmul(pb.t[:, 0:384], h.t[:, k, sub * 128:(sub + 1) * 128], wb.t[:, k, 1024:1408], start=(k == 0), stop=(k == 7)),
                                  R=[wb, h], W=[pb], acc=(k > 0))
                        tile = (g0 // 128) + sub
                        kb.op('act', lambda e: e.copy(Vaug.t[:, tile, :, 0:64], pa.t[:, 0:128].rearrange("p (h d) -> p h d", h=2)), R=[pa], W=[Vaug])
                        rw = rwsb[sub % 2]
                        kb.op('dve', lambda e: e.tensor_copy(rw.t[:, 0:384], pa.t[:, 128:512]), R=[pa], W=[rw])
                        kb.op('act', lambda e: e.copy(rw.t[:, 384:768], pb.t[:, 0:384]), R=[pb], W=[rw])
                        row0 = (1 if isctx else 259 - L) + g0 + sub * 128
                        kb.dma('pool', prw.t.ap()[row0:row0 + 128, :], rw.t[:], R=[rw], W=[prw])
            if debug:
                for r0 in range(0, PRW_ROWS, 1024):
                    r1 = min(PRW_ROWS, r0 + 1024)
                    kb.dma('sp', dbg['prw'][r0:r1, :], prw.t.ap()[r0:r1, :], R=[prw], out=True)

        if "na" in stages:
            with kb.phase():
                bias = kb.sb("bias", [128, 10, 640], F32)
                btmp = kb.sb("btmp", [128, 10, 640], F32)
                kb.dma('sp', bias.t[:], A['biasg'].rearrange("p (v n) -> p v n", v=10), W=[bias])
                kb.dma('pool', btmp.t[:], A['maskc'].rearrange("p (v n) -> p v n", v=10), W=[btmp])
                kb.op('pool', lambda e: e.tensor_tensor(bias.t[:], bias.t[:], btmp.t[:], ALU.add), R=[bias, btmp], W=[bias])
                psA = [kb.ps("psA%d" % i, [128, 512], F32) for i in range(2)]
                psB = [kb.ps("psB%d" % i, [128, 512], F32) for i in range(2)]
                pso = [kb.ps("pso%d" % i, [128, 512], F32) for i in range(2)]
                sA = [kb.sb("sA%d" % i, [128, 640], F32) for i in range(2)]
                PT = [kb.sb("PT%d" % i, [128, 896], BF16) for i in range(2)]
                rec = [kb.sb("rec%d" % i, [128, 1], F32) for i in range(2)]
                yb = [kb.sb("yb%d" % i, [128, 8, 128], BF16) for i in range(2)]
                it = 0
                for g in range(64):
                    m0 = min(max(g - 2, 0), 59)
                    var = {0: 0, 1: 1, 62: 3, 63: 4}.get(g, 2)
                    ybuf = yb[(g // 8) % 2]
                    for hd in range(2):
                        hp = slice(hd * 64, (hd + 1) * 64)
                        pA, pB, po = psA[it % 2], psB[it % 2], pso[it % 2]
                        s, P, rc = sA[it % 2], PT[it % 2], rec[it % 2]
                        it += 1
                        qs = slice(g * 128, (g + 1) * 128)
                        for j in range(5):
                            dst = pA.t[:, j * 128:(j + 1) * 128] if j < 4 else pB.t[:, 0:128]
                            kb.op('pe', lambda e: e.matmul(dst, KrT.t[hp, (m0 + j) * 128:(m0 + j + 1) * 128], QrT.t[hp, qs], start=True, stop=True),
                                  R=[KrT, QrT], W=[pA if j < 4 else pB], acc=(j not in (0, 4)))
                        for j in range(2):
                            kb.op('pe', lambda e: e.matmul(pB.t[:, (j + 1) * 128:(j + 2) * 128], KcT.t[hp, j * 128:(j + 1) * 128], QpT.t[hp, qs], start=True, stop=True),
                                  R=[KcT, QpT], W=[pB], acc=True)
                        bv = var * 2 + hd
                        kb.op('dve', lambda e: e.scalar_tensor_tensor(s.t[:, 0:512], pA.t[:], 0.125, bias.t[:, bv, 0:512], ALU.mult, ALU.add),
                              R=[pA, bias], W=[s])
                        kb.op('dve', lambda e: e.scalar_tensor_tensor(s.t[:, 512:640], pB.t[:, 0:128], 0.125, bias.t[:, bv, 512:640], ALU.mult, ALU.add),
                              R=[pB, bias], W=[s])
                        kb.op('act', lambda e: e.activation(P.t[:, 0:640], s.t[:], AF.Exp), R=[s], W=[P])
                        kb.op('act', lambda e: e.activation(P.t[:, 640:896], pB.t[:, 128:384], AF.Exp, scale=0.125), R=[pB], W=[P])
                        for j in range(7):
                            vt = (2 + m0 + j) if j < 5 else (j - 5)
                            kb.op('pe', lambda e: e.matmul(po.t[:, 0:65], P.t[:, j * 128:(j + 1) * 128], Vaug.t[:, vt, hd, 0:65], start=(j == 0), stop=(j == 6)),
                                  R=[P, Vaug], W=[po], acc=(j > 0))
                        kb.op('dve', lambda e: e.reciprocal(rc.t[:], po.t[:, 64:65]), R=[po], W=[rc])
                        kb.op('act', lambda e: e.activation(ybuf.t[:, g % 8, hd * 64:(hd + 1) * 64], po.t[:, 0:64], AF.Copy, scale=rc.t[:]),
                              R=[po, rc], W=[ybuf])
                    if g % 8 == 7:
                        g8 = g - 7
                        kb.dma('sp', ycat.t.ap()[g8 * 128:(g8 + 8) * 128, 0:128].rearrange("(j p) c -> p j c", p=128), ybuf.t[:], R=[ybuf], W=[ycat])

        if "inproj" in stages:
            kb.barrier()
            stA.close()
            kb.stack = outerA

        if "rwkv" in stages:
            stage_rwkv(kb, A, prw, ycat, identf, onesf)
        if debug:
            for r0 in range(0, T, 2048):
                kb.dma('sp', dbg['ycat'][r0:r0 + 2048, :], ycat.t.ap()[r0:r0 + 2048, :], R=[ycat], out=True)

        if debug and "rwkv" not in stages:
            for r0 in range(0, T, 2048):
                kb.dma('sp', ycat.t.ap()[r0:r0 + 2048, 128:256], dbg['yrw_in'][r0:r0 + 2048, :], W=[ycat])
        if "oproj" in stages:
            stage_oproj(kb, A, ycat, opart, modT, ident, identf, onesf)
        if "gather" in stages:
            kb.barrier()
            ccs = nc.alloc_semaphore("ccs")
            nc.gpsimd.collective_compute("ReduceScatter", ALU.add, replica_groups=[[0, 1, 2, 3], [4, 5, 6, 7]],
                                         ins=[opart.t.ap().opt()], outs=[osum.t.ap().opt()]).then_inc(ccs)
            pre = stage_b_pre(kb, A, modT, identf, onesf) if "b1" in stages else None
            for e in ['pe', 'act', 'dve', 'pool', 'sp']:
                kb.eng[e].wait_ge(ccs, 1)

        if "b1" in stages:
            stage_b(kb, A, osum, x1d, out, modT, ident, identf, onesf, dbg, nc, pre)
        kb.finish()
    return nc


def stage_rwkv(kb, A, prw, ycat, identf, onesf):
    CW = -0.6065306597126334
    import os as _os
    NLAT = int(_os.environ.get('KDEV_NLAT', '64'))
    st0 = ExitStack()
    old = kb.stack
    kb.stack = st0
    op = kb.op
    AXX = mybir.AxisListType.X
    rwv = kb.sb("rwv", [128, 4, 768], F32)
    for i in range(4):
        kb.dma('sp', rwv.t[:, i, :], A['rwv'][i:i + 1, :].partition_broadcast(128), W=[rwv])
    MUP = rwv.t[:, 0, :]
    MUN = rwv.t[:, 1, :]
    KKB = rwv.t[:, 2, 512:640]
    RKB = rwv.t[:, 3, 0:128]
    GNG = rwv.t[:, 3, 128:256]
    GNB = rwv.t[:, 3, 256:384]
    KA2 = rwv.t[:, 3, 384:640]
    onem = kb.sb("onem", [128, 768], F32)
    omk2 = kb.sb("omk2", [128, 256], F32)
    op('dve', lambda e: e.tensor_tensor(onem.t[:], MUP, MUN, ALU.add), R=[rwv], W=[onem])
    op('dve', lambda e: e.tensor_scalar(onem.t[:], onem.t[:], -1.0, 1.0, ALU.mult, ALU.add), R=[onem], W=[onem])
    op('dve', lambda e: e.tensor_scalar(omk2.t[:], KA2, -1.0, 1.0, ALU.mult, ALU.add), R=[rwv], W=[omk2])
    w2s = kb.sb("w2s", [128, 640], F32)
    kb.dma('sp', w2s.t[:, 0:256], A['rww2'], W=[w2s])
    kb.dma('sp', w2s.t[:, 256:512], A['rwa2'], W=[w2s])
    kb.dma('sp', w2s.t[:, 512:640], A['rwg2'], W=[w2s])
    tri = kb.sb("tri", [128, 4, 128], F32)
    kb.dma('sp', tri.t[:], A['tri'][:, 0:512].rearrange("p (m n) -> p m n", m=4), W=[tri])
    SU, SL, IU, IL = 0, 1, 2, 3
    mb = [kb.sb("mb%d" % d, [128, 5, 128], F32) for d in range(2)]
    for d, order in enumerate(((SU, IU, SU, IU, SL), (SL, IL, SL, IL, SU))):
        for i, m in enumerate(order):
            op('pool', lambda e: e.tensor_copy(mb[d].t[:, i, :], tri.t[:, m, :]), R=[tri], W=[mb[d]])
    Yst = [kb.sb("Yst%d" % d, [128, 64, 128], F32) for d in range(2)]

    class Lane:
        pass

    identr = kb.sb("identr", [128, 128], mybir.dt.float32r)
    op('pool', lambda e: e.tensor_copy(identr.t[:], identf.t[:]), R=[identf], W=[identr])
    zt = kb.sb("zt", [128, 512], F32)
    op('pool', lambda e: e.memset(zt.t[:], 0.0), W=[zt])

    lanes = []
    for d in range(2):
        ln = Lane()
        n = lambda nm: "%s_l%d" % (nm, d)
        ln.BF, ln.BA, ln.BB, ln.BC = [kb.ps(n("rp%d" % i), [128, 512], F32) for i in range(4)]
        ln.p0 = kb.sb(n("p0"), [128, 768], F32)
        ln.pm = kb.sb(n("pm"), [128, 768], F32)
        ln.pp = kb.sb(n("pp"), [128, 768], F32)
        ln.ps2 = [kb.sb(n("ps%d" % i), [128, 768], F32) for i in range(2)]
        ln.act3 = kb.sb(n("act3"), [128, 384], F32)
        ln.act3T = kb.sb(n("act3T"), [128, 384], F32)
        ln.tmpw = kb.sb(n("tmpw"), [128, 512], F32)
        ln.lw = kb.sb(n("lw"), [128, 256], F32)
        ln.aa = kb.sb(n("aa"), [128, 256], F32)
        ln.gg2 = [kb.sb(n("gg%d" % i), [128, 128], F32) for i in range(2)]
        ln.kq = kb.sb(n("kq"), [128, 128], F32)
        ln.sq = kb.sb(n("sq"), [128, 128], F32)
        ln.ss = kb.sb(n("ss"), [128, 2, 2], F32)
        ln.ss2 = kb.sb(n("ss2"), [128, 2], F32)
        ln.kk = kb.sb(n("kk"), [128, 128], F32)
        ln.kd = kb.sb(n("kd"), [128, 256], F32)
        ln.tk = kb.sb(n("tk"), [128, 256], F32)
        ln.rr = kb.sb(n("rr"), [128, 128], F32)
        ln.bs4 = kb.sb(n("bs4"), [128, 4], F32)
        ln.bsum2 = [kb.sb(n("bsum%d" % i), [128, 2, 2], F32) for i in range(2)]
        ln.lgs = kb.sb(n("lgs"), [128, 256], F32)
        ln.E = kb.sb(n("E"), [128, 4, 128], F32)
        ln.akk = kb.sb(n("akk"), [128, 128], F32)
        ln.Q62 = [kb.sb(n("Q6_%d" % i), [128, 6, 128], F32) for i in range(2)]
        ln.QT = kb.sb(n("QT"), [64, 2, 4, 128], mybir.dt.float32r)
        ln.gcol2 = [kb.sb(n("gcol%d" % i), [64, 2, 2], F32) for i in range(2)]
        ln.G = kb.sb(n("G"), [128, 2, 3, 128], F32)
        ln.Gk = kb.sb(n("Gk"), [128, 2, 128], F32)
        FRD = mybir.dt.float32r if _os.environ.get('KDEV_F32R', '1') == '1' else F32
        ln.C = [[kb.sb(n("C%d_%d" % (i, h)), [128, 4, 128], FRD) for h in range(2)] for i in range(2)]
        for i in range(2):
            for h in range(2):
                op('pool', lambda e: e.tensor_copy(ln.C[i][h].t[:].rearrange("p a t -> p (a t)"), zt.t[:]), R=[zt], W=[ln.C[i][h]])
        ln.Xs = kb.sb(n("Xs"), [128, 128], F32)
        ln.Us = kb.sb(n("Us"), [128, 128], F32)
        ln.yv = kb.sb(n("yv"), [128, 128], F32)
        ln.gst = kb.sb(n("gst"), [128, 2, 6], F32)
        ln.gmv = kb.sb(n("gmv"), [128, 2, 2], F32)
        ln.grs = kb.sb(n("grs"), [128, 2, 2], F32)
        ln.gnm = kb.sb(n("gnm"), [128, 2, 2], F32)
        ln.yo = [kb.sb(n("yo%d" % i), [128, 128], BF16) for i in range(2)]
        ln.Z = kb.sb(n("Z"), [64, 128], F32)
        ln.nout = 0
        lanes.append(ln)

    def unpack(ln, vis):
        row0, d, is_ctx, chunk, readout, par = vis
        return dict(p0=ln.p0, pm=ln.pm, pp=ln.pp, ps=ln.ps2[par], act3=ln.act3, act3T=ln.act3T, tmpw=ln.tmpw, lw=ln.lw, aa=ln.aa,
                    gg=ln.gg2[par], kq=ln.kq, sq=ln.sq, ss=ln.ss, ss2=ln.ss2, kk=ln.kk, kd=ln.kd, tk=ln.tk, rr=ln.rr, bs4=ln.bs4,
                    bsum=ln.bsum2[par], lgs=ln.lgs, E=ln.E, akk=ln.akk, Q6=ln.Q62[par], QT=ln.QT, gcol=ln.gcol2[par], G=ln.G, Gk=ln.Gk,
                    Xs=ln.Xs, Us=ln.Us, yv=ln.yv, Z=ln.Z)

    def front(ln, vis):
        row0, d, is_ctx, chunk, readout, par = vis
        t_ = unpack(ln, vis)
        p0, pm, pp, ps, act3, act3T, tmpw, lw, aa, gg = (t_[k] for k in ('p0', 'pm', 'pp', 'ps', 'act3', 'act3T', 'tmpw', 'lw', 'aa', 'gg'))
        kq, sq, ss, ss2, kk, kd, tk, rr = (t_[k] for k in ('kq', 'sq', 'ss', 'ss2', 'kk', 'kd', 'tk', 'rr'))
        bs4, bsum, lgs, E, akk, Q6, gcol = (t_[k] for k in ('bs4', 'bsum', 'lgs', 'E', 'akk', 'Q6', 'gcol'))
        B0 = B1 = B2 = ln.BF
        dq = 'sp'
        kb.dma(dq, p0.t[:], prw.t.ap()[row0:row0 + 128, :], R=[prw], W=[p0])
        kb.dma(dq, pm.t[:], prw.t.ap()[row0 - 1:row0 + 127, :], R=[prw], W=[pm])
        kb.dma(dq, pp.t[:], prw.t.ap()[row0 + 1:row0 + 129, :], R=[prw], W=[pp])
        yield
        op('dve', lambda e: e.tensor_tensor(ps.t[:], p0.t[:], onem.t[:], ALU.mult), R=[p0, onem], W=[ps])
        op('pool', lambda e: e.tensor_tensor(pm.t[:], pm.t[:], MUP, ALU.mult), R=[pm, rwv], W=[pm])
        op('pool', lambda e: e.tensor_tensor(pp.t[:], pp.t[:], MUN, ALU.mult), R=[pp, rwv], W=[pp])
        yield
        op('dve', lambda e: e.tensor_tensor(ps.t[:], ps.t[:], pm.t[:], ALU.add), R=[ps, pm], W=[ps])
        op('dve', lambda e: e.tensor_tensor(ps.t[:], ps.t[:], pp.t[:], ALU.add), R=[ps, pp], W=[ps])
        yield
        r_ = ps.t[:, 0:128]
        k_ = ps.t[:, 128:256]
        vh = lambda h: ps.t[:, 256 + h * 64:256 + (h + 1) * 64]
        op('act', lambda e: e.activation(act3.t[:, 0:128], ps.t[:, 384:512], AF.Tanh), R=[ps], W=[act3])
        op('act', lambda e: e.activation(act3.t[:, 256:384], ps.t[:, 640:768], AF.Sigmoid), R=[ps], W=[act3])
        op('pool', lambda e: e.tensor_copy(act3.t[:, 128:256], ps.t[:, 512:640]), R=[ps], W=[act3])
        yield
        for i in range(3):
            op('pe', lambda e: e.transpose(B0.t[:, i * 128:(i + 1) * 128], act3.t[:, i * 128:(i + 1) * 128], identf.t[:]),
               R=[act3, identf], W=[B0], acc=(i > 0))
        yield
        op('act', lambda e: e.copy(act3T.t[:], B0.t[:, 0:384]), R=[B0], W=[act3T])
        yield
        for i in range(2):
            op('pe', lambda e: e.matmul(B1.t[:, i * 256:(i + 1) * 256], act3T.t[:, i * 128:(i + 1) * 128],
                                        w2s.t[:, i * 256:(i + 1) * 256], start=True, stop=True),
               R=[act3T, w2s], W=[B1], acc=(i > 0))
        yield
        op('dve', lambda e: e.tensor_tensor(tmpw.t[:], B1.t[:], rwv.t[:, 2, 0:512], ALU.add), R=[B1, rwv], W=[tmpw])
        yield
        op('act', lambda e: e.activation(tmpw.t[:], tmpw.t[:], AF.Sigmoid), R=[tmpw], W=[tmpw])
        yield
        op('dve', lambda e: e.tensor_scalar_mul(lw.t[:], tmpw.t[:, 0:256], CW), R=[tmpw], W=[lw])
        op('pool', lambda e: e.tensor_copy(aa.t[:], tmpw.t[:, 256:512]), R=[tmpw], W=[aa])
        yield
        lwd = lw.t[:, d * 128:(d + 1) * 128]
        tmat = tri.t[:, IU if d == 0 else IL, :]
        op('pe', lambda e: e.matmul(B2.t[:, 128:256], tmat, lwd, start=True, stop=True), R=[tri, lw], W=[B2])
        op('pe', lambda e: e.matmul(B2.t[:, 256:384], onesf.t[:], lwd, start=True, stop=True), R=[onesf, lw], W=[B2], acc=True)
        for h in range(2):
            op('pe', lambda e: e.matmul(B0.t[0:64, 384 + 2 * h:385 + 2 * h], lw.t[:, d * 128 + h * 64: d * 128 + (h + 1) * 64], onesf.t[:, 0:1],
                                        start=True, stop=True), R=[lw, onesf], W=[B0], acc=True)
        if readout:
            op('pe', lambda e: e.matmul(B2.t[:, 0:128], act3T.t[:, 256:384], w2s.t[:, 512:640], start=True, stop=True),
               R=[act3T, w2s], W=[B2], acc=True)
        yield
        if readout:
            op('act', lambda e: e.copy(gg.t[:], B2.t[:, 0:128]), R=[B2], W=[gg])
        op('dve', lambda e: e.tensor_tensor(kq.t[:], k_, KKB, ALU.mult), R=[ps, rwv], W=[kq])
        op('pool', lambda e: e.tensor_tensor(tk.t[:], aa.t[:], KA2, ALU.mult), R=[aa, rwv], W=[tk])
        yield
        op('pool', lambda e: e.tensor_tensor(sq.t[:], kq.t[:], kq.t[:], ALU.mult), R=[kq], W=[sq])
        op('pool', lambda e: e.tensor_tensor(tk.t[:], tk.t[:], omk2.t[:], ALU.add), R=[tk, omk2], W=[tk])
        yield
        op('dve', lambda e: e.tensor_reduce(ss2.t[:], sq.t[:].rearrange("p (g j) -> p g j", j=64), AXX, ALU.add), R=[sq], W=[ss2])
        for dd in range(2):
            op('pool', lambda e: e.tensor_tensor(kd.t[:, dd * 128:(dd + 1) * 128], tk.t[:, dd * 128:(dd + 1) * 128], k_, ALU.mult),
               R=[tk, ps], W=[kd])
        yield
        op('act', lambda e: e.activation(ss2.t[:], ss2.t[:], AF.Sqrt), R=[ss2], W=[ss2])
        yield
        op('dve', lambda e: e.tensor_scalar_max(ss2.t[:], ss2.t[:], 1e-12), R=[ss2], W=[ss2])
        op('dve', lambda e: e.reciprocal(ss2.t[:], ss2.t[:]), R=[ss2], W=[ss2])
        op('dve', lambda e: e.tensor_copy(ss.t[:, :, 0:1], ss2.t[:].rearrange("p (h o) -> p h o", o=1)), R=[ss2], W=[ss])
        yield
        for h in range(2):
            op('dve', lambda e: e.tensor_scalar_mul(kk.t[:, h * 64:(h + 1) * 64], kq.t[:, h * 64:(h + 1) * 64], ss.t[:, h, 0:1]),
               R=[kq, ss], W=[kk])
        yield
        if readout:
            op('pool', lambda e: e.tensor_tensor(rr.t[:], r_, RKB, ALU.mult), R=[ps, rwv], W=[rr])
            for dd in range(2):
                op('pool', lambda e: e.tensor_tensor(tk.t[:, dd * 128:(dd + 1) * 128], kd.t[:, dd * 128:(dd + 1) * 128], rr.t[:], ALU.mult),
                   R=[kd, rr], W=[tk])
            yield
            op('dve', lambda e: e.tensor_reduce(bs4.t[:], tk.t[:].rearrange("p (g j) -> p g j", j=64), AXX, ALU.add),
               R=[tk], W=[bs4])
            op('dve', lambda e: e.tensor_tensor(bsum.t[:, :, 0:1], bs4.t[:, 0:2].rearrange("p (h o) -> p h o", o=1),
                                                bs4.t[:, 2:4].rearrange("p (h o) -> p h o", o=1), ALU.add), R=[bs4], W=[bsum])
            yield
        op('act', lambda e: e.activation(gcol.t[:, :, 0:1], B0.t[0:64, 384:388].rearrange("p (h o) -> p h o", o=2)[:, :, 0:1], AF.Exp),
           R=[B0], W=[gcol])
        op('dve', lambda e: e.tensor_copy(lgs.t[:], B2.t[:, 128:384]), R=[B2], W=[lgs])
        yield
        lg = lgs.t[:, 0:128]
        lgC = lgs.t[:, 128:256]
        op('act', lambda e: e.activation(E.t[:, 0, :], lg, AF.Exp), R=[lgs], W=[E])
        op('act', lambda e: e.activation(E.t[:, 1, :], lg, AF.Exp, scale=-1.0), R=[lgs], W=[E])
        op('dve', lambda e: e.tensor_tensor(E.t[:, 2, :], lg, lwd, ALU.subtract), R=[lgs, lw], W=[E])
        op('dve', lambda e: e.tensor_tensor(E.t[:, 3, :], lgC, lg, ALU.subtract), R=[lgs], W=[E])
        yield
        op('act', lambda e: e.activation(E.t[:, 2:4, :], E.t[:, 2:4, :], AF.Exp), R=[E], W=[E])
        ad = aa.t[:, d * 128:(d + 1) * 128]
        kdd = kd.t[:, d * 128:(d + 1) * 128]
        op('pool', lambda e: e.tensor_tensor(akk.t[:], ad, kk.t[:], ALU.mult), R=[aa, kk], W=[akk])
        yield
        op('dve', lambda e: e.tensor_tensor(Q6.t[:, 0, :], r_, E.t[:, 0, :], ALU.mult), R=[ps, E], W=[Q6])
        op('dve', lambda e: e.tensor_tensor(Q6.t[:, 1, :], akk.t[:], E.t[:, 1, :], ALU.mult), R=[akk, E], W=[Q6])
        op('pool', lambda e: e.tensor_tensor(Q6.t[:, 2, :], kdd, E.t[:, 1, :], ALU.mult), R=[kd, E], W=[Q6])
        yield
        op('dve', lambda e: e.scalar_tensor_tensor(Q6.t[:, 3, :], kk.t[:], -1.0, E.t[:, 2, :], ALU.mult, ALU.mult), R=[kk, E], W=[Q6])
        op('pool', lambda e: e.tensor_tensor(Q6.t[:, 4, :], akk.t[:], E.t[:, 3, :], ALU.mult), R=[akk, E], W=[Q6])
        op('pool', lambda e: e.tensor_tensor(Q6.t[:, 5, :], kdd, E.t[:, 3, :], ALU.mult), R=[kd, E], W=[Q6])
        yield
        return

    def back(ln, vis):
        row0, d, is_ctx, chunk, readout, par = vis
        t_ = unpack(ln, vis)
        ps, gg, bsum, Q6, QT, gcol, G, Gk, Xs, Us, yv, Z = (t_[k] for k in ('ps', 'gg', 'bsum', 'Q6', 'QT', 'gcol', 'G', 'Gk', 'Xs', 'Us', 'yv', 'Z'))
        vh = lambda h: ps.t[:, 256 + h * 64:256 + (h + 1) * 64]
        dq = 'sp'
        BA, BB, BC = ln.BA, ln.BB, ln.BC
        src = (3, 0, 1, 2)
        tb = (BA, BB)
        for h in range(2):
            for i in range(4):
                op('pe', lambda e: e.transpose(tb[h].t[0:64, i * 128:(i + 1) * 128], Q6.t[:, src[i], h * 64:(h + 1) * 64], identf.t[:]),
                   R=[Q6, identf], W=[tb[h]], acc=(i > 0))
        yield
        op('act', lambda e: e.copy(QT.t[:, 0, :, :].rearrange("p a t -> p (a t)"), BA.t[0:64, :]), R=[BA], W=[QT])
        op('dve', lambda e: e.tensor_copy(QT.t[:, 1, :, :].rearrange("p a t -> p (a t)"), BB.t[0:64, :]), R=[BB], W=[QT])
        yield
        gb = (BA, BB)
        for h in range(2):
            AR = QT.t[:, h, 0:2, :].rearrange("p a t -> p (a t)")
            BK = QT.t[:, h, 2:4, :].rearrange("p a t -> p (a t)")
            op('pe', lambda e: e.matmul(gb[h].t[:, 0:256], QT.t[:, h, 2, :], AR, start=True, stop=True), R=[QT], W=[gb[h]])
            op('pe', lambda e: e.matmul(gb[h].t[:, 256:512], QT.t[:, h, 3, :], AR, start=True, stop=True), R=[QT], W=[gb[h]], acc=True)
            op('pe', lambda e: e.matmul(BC.t[:, h * 256:(h + 1) * 256], QT.t[:, h, 0, :], BK, start=True, stop=True), R=[QT], W=[BC], acc=(h > 0))
        yield
        C = ln.C[0]
        for h in range(2):
            op('dve', lambda e: e.tensor_tensor(C[h].t[:, 0, :], gb[h].t[:, 0:128], mb[d].t[:, 0, :], ALU.mult), R=[gb[h], mb[d]], W=[C[h]])
            op('dve', lambda e: e.tensor_tensor(C[h].t[:, 3, :], BC.t[:, h * 256:h * 256 + 128], mb[d].t[:, 4, :], ALU.mult), R=[BC, mb[d]], W=[C[h]])
            op('pool', lambda e: e.tensor_copy(C[h].t[:, 1, :], identf.t[:]), R=[identf], W=[C[h]])
            yield
        for h in range(2):
            op('dve', lambda e: e.tensor_tensor(G.t[:, h, :, :].rearrange("p a t -> p (a t)"), gb[h].t[:, 128:512],
                                                mb[d].t[:, 1:4, :].rearrange("p a t -> p (a t)"), ALU.mult), R=[gb[h], mb[d]], W=[G])
        yield
        ib = (BA, BB)
        for lev in range(7):
            Cn = ln.C[(lev + 1) % 2]
            last = (lev == 6)
            for h in range(2):
                op('pe', lambda e: e.matmul(ib[h].t[:, 0:256], C[h].t[:, 3, :], C[h].t[:, 0:2, :].rearrange("p a t -> p (a t)"), start=True, stop=False),
                   R=[C[h]], W=[ib[h]])
                op('pe', lambda e: e.matmul(ib[h].t[:, 128:256], identr.t[:], C[h].t[:, 1, :], start=False, stop=True),
                   R=[C[h], identr], W=[ib[h]], acc=True)
                if not last:
                    op('pe', lambda e: e.matmul(ib[h].t[:, 256:512], C[h].t[:, 0, :], C[h].t[:, 2:4, :].rearrange("p a t -> p (a t)"), start=True, stop=True),
                       R=[C[h]], W=[ib[h]], acc=True)
            yield
            op('act', lambda e: e.copy(Cn[0].t[:].rearrange("p a t -> p (a t)"), ib[0].t[:]), R=[ib[0]], W=[Cn[0]])
            op('dve', lambda e: e.tensor_copy(Cn[1].t[:].rearrange("p a t -> p (a t)"), ib[1].t[:]), R=[ib[1]], W=[Cn[1]])
            yield
            C = Cn
        TTh = [C[h].t[:, 1, :].bitcast(F32) for h in range(2)]
        TTb = [C[0], C[1]]
        for h in range(2):
            hs = slice(h * 64, (h + 1) * 64)
            op('pe', lambda e: e.matmul(BC.t[:, hs], QT.t[:, h, 0, :].bitcast(F32), Z.t[:, hs], start=True, stop=False), R=[QT, Z], W=[BC], acc=(h > 0))
            op('pe', lambda e: e.matmul(BC.t[:, hs], G.t[:, h, 1, :], vh(h), start=False, stop=True), R=[G, ps], W=[BC], acc=True)
        yield
        op('act', lambda e: e.copy(Xs.t[:], BC.t[:, 0:128]), R=[BC], W=[Xs])
        yield
        for h in range(2):
            hs = slice(h * 64, (h + 1) * 64)
            op('pe', lambda e: e.matmul(BC.t[:, 128 + h * 64:128 + (h + 1) * 64], TTh[h], Xs.t[:, hs], start=True, stop=True), R=[TTb[h], Xs], W=[BC], acc=(h > 0))
        yield
        op('dve', lambda e: e.tensor_copy(Us.t[:], BC.t[:, 128:256]), R=[BC], W=[Us])
        yield
        first = True
        if not is_ctx:
            for h in range(2):
                hs = slice(h * 64, (h + 1) * 64)
                ys = slice(256 + h * 64, 256 + (h + 1) * 64)
                op('pe', lambda e: e.matmul(BC.t[:, ys], QT.t[:, h, 1, :].bitcast(F32), Z.t[:, hs], start=True, stop=False), R=[QT, Z], W=[BC], acc=(not first))
                first = False
                op('pe', lambda e: e.matmul(BC.t[:, ys], G.t[:, h, 0, :], Us.t[:, hs], start=False, stop=False), R=[G, Us], W=[BC], acc=True)
                op('pe', lambda e: e.matmul(BC.t[:, ys], G.t[:, h, 2, :], vh(h), start=False, stop=True), R=[G, ps], W=[BC], acc=True)
        for h in range(2):
            hs = slice(h * 64, (h + 1) * 64)
            zs = slice(384 + h * 64, 384 + (h + 1) * 64)
            op('pe', lambda e: e.matmul(BC.t[0:64, zs], Q6.t[:, 4, hs], Us.t[:, hs], start=True, stop=False), R=[Q6, Us], W=[BC], acc=(not first))
            first = False
            op('pe', lambda e: e.matmul(BC.t[0:64, zs], Q6.t[:, 5, hs], vh(h), start=False, stop=True), R=[Q6, ps], W=[BC], acc=True)
        yield
        for h in range(2):
            hs = slice(h * 64, (h + 1) * 64)
            op('dve', lambda e: e.scalar_tensor_tensor(Z.t[:, hs], Z.t[:, hs], gcol.t[:, h, 0:1], BC.t[0:64, 384 + h * 64:384 + (h + 1) * 64], ALU.mult, ALU.add),
               R=[Z, gcol, BC], W=[Z])
        yield
        if is_ctx:
            return
        if not readout:
            op('act', lambda e: e.copy(Yst[d].t[:, chunk, :], BC.t[:, 256:384]), R=[BC], W=[Yst[d]])
            yield
            return
        op('dve', lambda e: e.tensor_tensor(yv.t[:], BC.t[:, 256:384], Yst[1 - d].t[:, chunk, :], ALU.add), R=[BC, Yst[1 - d]], W=[yv])
        yield
        gst, gmv, grs, gnm = ln.gst, ln.gmv, ln.grs, ln.gnm
        for h in range(2):
            op('dve', lambda e: e.bn_stats(gst.t[:, h, :], yv.t[:, h * 64:(h + 1) * 64]), R=[yv], W=[gst])
            op('dve', lambda e: e.bn_aggr(gmv.t[:, h, :], gst.t[:, h, :]), R=[gst], W=[gmv])
        yield
        op('act', lambda e: e.activation(grs.t[:, :, 0:1], gmv.t[:, :, 1:2], AF.Sqrt, bias=64e-5, scale=1.0), R=[gmv], W=[grs])
        yield
        op('dve', lambda e: e.reciprocal(grs.t[:, :, 0:1], grs.t[:, :, 0:1]), R=[grs], W=[grs])
        op('dve', lambda e: e.scalar_tensor_tensor(gnm.t[:, :, 0:1], gmv.t[:, :, 0:1], -1.0, grs.t[:, :, 0:1], ALU.mult, ALU.mult), R=[gmv, grs], W=[gnm])
        yield
        for h in range(2):
            hs = slice(h * 64, (h + 1) * 64)
            op('act', lambda e: e.activation(yv.t[:, hs], yv.t[:, hs], AF.Identity, bias=gnm.t[:, h, 0:1], scale=grs.t[:, h, 0:1]), R=[yv, gnm, grs], W=[yv])
        yield
        op('pool', lambda e: e.tensor_tensor(yv.t[:], yv.t[:], GNG, ALU.mult), R=[yv, rwv], W=[yv])
        op('pool', lambda e: e.tensor_tensor(yv.t[:], yv.t[:], GNB, ALU.add), R=[yv, rwv], W=[yv])
        yield
        for h in range(2):
            hs = slice(h * 64, (h + 1) * 64)
            op('dve', lambda e: e.scalar_tensor_tensor(yv.t[:, hs], vh(h), bsum.t[:, h, 0:1], yv.t[:, hs], ALU.mult, ALU.add),
               R=[ps, bsum, yv], W=[yv])
        y_ = ln.yo[ln.nout % 2]
        ln.nout += 1
        op('dve', lambda e: e.tensor_tensor(y_.t[:], yv.t[:], gg.t[:], ALU.mult), R=[yv, gg], W=[y_])
        kb.dma(dq, ycat.t.ap()[chunk * 128:(chunk + 1) * 128, 128:256], y_.t[:], R=[y_], W=[ycat])
        yield

    def lane_visits(d):
        half = NLAT // 2
        vs = []
        corder = (0, 1) if d == 0 else (1, 0)
        for c in corder:
            vs.append((1 + 128 * c, d, True, c, False))
        lorder = range(NLAT) if d == 0 else range(NLAT - 1, -1, -1)
        for n_ in lorder:
            ro = (n_ >= half) if d == 0 else (n_ < half)
            vs.append((259 + 128 * n_, d, False, n_, ro))
        return [v + (i % 2,) for i, v in enumerate(vs)]

    def interleave(g1, g2):
        a1 = a2 = True
        while a1 or a2:
            if a1:
                try:
                    next(g1)
                except StopIteration:
                    a1 = False
            if a2:
                try:
                    next(g2)
                except StopIteration:
                    a2 = False
            yield

    def lane_gen(d):
        ln = lanes[d]
        op('dve', lambda e: e.memset(ln.Z.t[:], 0.0), W=[ln.Z])
        vs = lane_visits(d)
        yield from front(ln, vs[0])
        for i in range(len(vs)):
            if i + 1 < len(vs):
                yield from interleave(back(ln, vs[i]), front(ln, vs[i + 1]))
            else:
                yield from back(ln, vs[i])

    gens = [lane_gen(0), lane_gen(1)]
    alive = [True, True]
    while any(alive):
        for i in range(2):
            if alive[i]:
                try:
                    next(gens[i])
                except StopIteration:
                    alive[i] = False
    kb.barrier()
    st0.close()
    kb.stack = old


def bcast_mod(kb, dst, modT, j0, identf, onesf):
    with kb.phase():
        dg = [kb.sb("dg%d" % i, [128, 128], F32) for i in range(2)]
        pb_ = [kb.ps("pbc%d" % i, [128, 512], F32) for i in range(2)]
        for k in range(8):
            d_ = dg[k % 2]
            kb.op('dve', lambda e: e.tensor_scalar_mul(d_.t[:], identf.t[:], modT.t[:, j0 + k, 0:1]), R=[identf, modT], W=[d_])
            p = pb_[k % 2]
            kb.op('pe', lambda e: e.matmul(p.t[:, 0:128], onesf.t[:], d_.t[:], start=True, stop=True), R=[onesf, d_], W=[p])
            kb.op('act', lambda e: e.copy(dst.t[:, k * 128:(k + 1) * 128], p.t[:, 0:128]), R=[p], W=[dst])


def stage_oproj(kb, A, ycat, opart, modT, ident, identf, onesf):
    with kb.phase():
        g1t = kb.sb("g1t", [128, D], F32)
        bcast_mod(kb, g1t, modT, 16, identf, onesf)
        wof = kb.sb("wof", [128, 2, D], F32)
        wob = kb.sb("wob", [128, 2, D], BF16)
        kb.dma('sp', wof.t[:], A['wo'].rearrange("(k p) n -> p k n", p=128), W=[wof])
        for k in range(2):
            kb.op('pool', lambda e: e.tensor_tensor(wob.t[:, k, :], wof.t[:, k, :], g1t.t[:], ALU.mult), R=[wof, g1t], W=[wob])
        ysb = [kb.sb("ysb%d" % i, [128, 4, 256], BF16) for i in range(2)]
        yT = [kb.sb("yT%d" % i, [128, 2, 512], BF16) for i in range(2)]
        ptr = [kb.ps("optr%d" % i, [128, 1024], BF16) for i in range(2)]
        po = [kb.ps("opo%d" % i, [128, 512], F32) for i in range(4)]
        ob = [kb.sb("ob%d" % i, [128, D], F32) for i in range(2)]
        cnt = 0
        for blk in range(16):
            ys = ysb[blk % 2]
            yt_ = yT[blk % 2]
            pp = ptr[blk % 2]
            kb.dma('sp', ys.t[:], ycat.t.ap()[blk * 512:(blk + 1) * 512, :].rearrange("(s p) c -> p s c", p=128), R=[ycat], W=[ys])
            for k in range(2):
                for sub in range(4):
                    kb.op('pe', lambda e: e.transpose(pp.t[:, k * 512 + sub * 128: k * 512 + (sub + 1) * 128], ys.t[:, sub, k * 128:(k + 1) * 128], ident.t[:]),
                          R=[ys, ident], W=[pp], acc=(k + sub > 0))
            kb.op('act', lambda e: e.copy(yt_.t[:].rearrange("p k t -> p (k t)"), pp.t[:]), R=[pp], W=[yt_])
            for sub in range(4):
                o = ob[cnt % 2]
                for half in range(2):
                    p = po[(cnt % 2) * 2 + half]
                    for k in range(2):
                        kb.op('pe', lambda e: e.matmul(p.t[:], yt_.t[:, k, sub * 128:(sub + 1) * 128], wob.t[:, k, half * 512:(half + 1) * 512], start=(k == 0), stop=(k == 1)),
                              R=[yt_, wob], W=[p], acc=(k > 0))
                    if half == 0:
                        kb.op('act', lambda e: e.copy(o.t[:, 0:512], p.t[:]), R=[p], W=[o])
                    else:
                        kb.op('dve', lambda e: e.tensor_copy(o.t[:, 512:1024], p.t[:]), R=[p], W=[o])
                tok0 = blk * 512 + sub * 128
                kb.dma('pool', opart.t.ap()[tok0:tok0 + 128, :], o.t[:], R=[o], W=[opart])
                cnt += 1


def stage_b_pre(kb, A, modT, identf, onesf):
    stB = ExitStack()
    old = kb.stack
    kb.stack = stB
    h2T = kb.sb("h2T", [128, 8, 2048], BF16)
    w1b = kb.sb("w1b", [128, 8, 4 * D], BF16)
    w2b = kb.sb("w2b", [128, 32, D], BF16)
    with kb.phase():
        g2t = kb.sb("g2t", [128, D], F32)
        bcast_mod(kb, g2t, modT, 40, identf, onesf)
        ws = [kb.sb("w12st%d" % i, [128, 1024], F32) for i in range(4)]
        n = 0
        for k in range(8):
            for hh in range(4):
                w = ws[n % 4]
                kb.dma('sp', w.t[:], A['w1'][k * 128:(k + 1) * 128, hh * 1024:(hh + 1) * 1024], W=[w])
                eng = ('pool', 'dve', 'act')[n % 3]
                if eng == 'act':
                    kb.op('act', lambda e: e.copy(w1b.t[:, k, hh * 1024:(hh + 1) * 1024], w.t[:]), R=[w], W=[w1b])
                else:
                    kb.op(eng, lambda e: e.tensor_copy(w1b.t[:, k, hh * 1024:(hh + 1) * 1024], w.t[:]), R=[w], W=[w1b])
                n += 1
        for f in range(32):
            w = ws[n % 4]
            kb.dma('sp', w.t[:], A['w2'][f * 128:(f + 1) * 128, :], W=[w])
            kb.op(('pool', 'dve')[n % 2], lambda e: e.tensor_tensor(w2b.t[:, f, :], w.t[:], g2t.t[:], ALU.mult), R=[w, g2t], W=[w2b])
            n += 1
    return dict(stB=stB, old=old, h2T=h2T, w1b=w1b, w2b=w2b)


def stage_b(kb, A, osum, x1d, out, modT, ident, identf, onesf, dbg, nc, pre):
    stB, old, h2T, w1b, w2b = pre['stB'], pre['old'], pre['h2T'], pre['w1b'], pre['w2b']

    with kb.phase():
        lnb = kb.sb("lnb1", [128, 2, D], F32)
        for i in range(2):
            kb.dma('sp', lnb.t[:, i, :], A['lnv'][i:i + 1, :].partition_broadcast(128), W=[lnb])
        ptr = [kb.ps("bptr%d" % i, [128, 1024], BF16) for i in range(2)]
        xt = [kb.sb("bxt%d" % i, [128, D], F32) for i in range(2)]
        ot = [kb.sb("bot%d" % i, [128, D], F32) for i in range(2)]
        xn2 = [kb.sb("xn2_%d" % i, [128, D], BF16) for i in range(2)]
        st_ = [kb.sb("bst%d" % i, [128, 2, 6], F32) for i in range(4)]
        mv_ = [kb.sb("bmv%d" % i, [128, 2], F32) for i in range(4)]
        rstd_ = [kb.sb("brstd%d" % i, [128, 2], F32) for i in range(4)]
        nmr_ = [kb.sb("bnmr%d" % i, [128, 2], F32) for i in range(4)]
        for cnt in range(16):
            tok0 = cnt * 128
            xx = xt[cnt % 2]; oo = ot[cnt % 2]; xp = oo; x1 = xx; xb = xn2[cnt % 2]
            pp = ptr[cnt % 2]
            kb.dma('sp', xx.t[:], A['xb'][tok0:tok0 + 128, :], W=[xx])
            kb.dma('sp', oo.t[:], osum.t.ap()[tok0:tok0 + 128, :], R=[osum], W=[oo])
            kb.op('dve', lambda e: e.scalar_tensor_tensor(xp.t[:], xx.t[:], ALPHA, oo.t[:], ALU.mult, ALU.add), R=[xx, oo], W=[xp])
            st, mv, rstd, nmr = (z[(cnt % 2) * 2] for z in (st_, mv_, rstd_, nmr_))
            ln_stats(kb, xp, xp.t, st, mv, rstd, nmr, 1e-6)
            kb.op('act', lambda e: e.activation(x1.t[:], xp.t[:], AF.Identity, bias=nmr.t[:, 0:1], scale=rstd.t[:, 0:1]), R=[xp, nmr, rstd], W=[x1])
            kb.op('pool', lambda e: e.tensor_tensor(x1.t[:], x1.t[:], lnb.t[:, 0, :], ALU.mult), R=[x1, lnb], W=[x1])
            kb.op('pool', lambda e: e.tensor_tensor(x1.t[:], x1.t[:], lnb.t[:, 1, :], ALU.add), R=[x1, lnb], W=[x1])
            kb.dma('pool', x1d.t.ap()[tok0:tok0 + 128, :], x1.t[:], R=[x1], W=[x1d])
            st, mv, rstd, nmr = (z[(cnt % 2) * 2 + 1] for z in (st_, mv_, rstd_, nmr_))
            ln_stats(kb, x1, x1.t, st, mv, rstd, nmr, 1e-6)
            kb.op('act', lambda e: e.activation(xb.t[:], x1.t[:], AF.Identity, bias=nmr.t[:, 0:1], scale=rstd.t[:, 0:1]), R=[x1, nmr, rstd], W=[xb])
            for k in range(8):
                kb.op('pe', lambda e: e.transpose(pp.t[:, k * 128:(k + 1) * 128], xb.t[:, k * 128:(k + 1) * 128], ident.t[:]),
                      R=[xb, ident], W=[pp], acc=(k > 0))
            for k in range(8):
                kb.op('act', lambda e: e.activation(h2T.t[:, k, tok0:tok0 + 128], pp.t[:, k * 128:(k + 1) * 128],
                                                    AF.Identity, bias=modT.t[:, 24 + k, 0:1], scale=modT.t[:, 32 + k, 0:1]),
                      R=[pp, modT], W=[h2T])
        if 'x1' in dbg:
            for r0 in range(0, 2048, 512):
                kb.dma('sp', dbg['x1'][r0:r0 + 512, :], x1d.t.ap()[r0:r0 + 512, :], R=[x1d], out=True)

    with kb.phase():
        lnb = kb.sb("lnb2", [128, 2, D], F32)
        for i in range(2):
            kb.dma('sp', lnb.t[:, i, :], A['lnv'][2 + i:3 + i, :].partition_broadcast(128), W=[lnb])
        pu = [kb.ps("pu%d" % i, [128, 512], F32) for i in range(2)]
        pacc = [kb.ps("pacc%d" % i, [128, 512], F32) for i in range(4)]
        rl = [kb.sb("rl%d" % i, [128, 256], F32) for i in range(2)]
        h1 = [kb.sb("h1_%d" % i, [128, 256], BF16) for i in range(3)]
        x1 = [kb.sb("cx1_%d" % i, [128, D], F32) for i in range(2)]
        xp = [kb.sb("cxp_%d" % i, [128, D], F32) for i in range(2)]
        ot = [kb.sb("cot_%d" % i, [128, D], F32) for i in range(2)]
        st_ = [kb.sb("cst%d" % i, [128, 2, 6], F32) for i in range(2)]
        mv_ = [kb.sb("cmv%d" % i, [128, 2], F32) for i in range(2)]
        rstd_ = [kb.sb("crstd%d" % i, [128, 2], F32) for i in range(2)]
        nmr_ = [kb.sb("cnmr%d" % i, [128, 2], F32) for i in range(2)]
        cnt = 0
        for blk in range(8):
            t0 = blk * 256
            for f in range(32):
                u = pu[f % 2]
                for k in range(8):
                    kb.op('pe', lambda e: e.matmul(u.t[:, 0:256], w1b.t[:, k, f * 128:(f + 1) * 128], h2T.t[:, k, t0:t0 + 256], start=(k == 0), stop=(k == 7)),
                          R=[w1b, h2T], W=[u], acc=(k > 0))
                rr = rl[f % 2]
                hh = h1[f % 3]
                kb.op('act', lambda e: e.activation(rr.t[:], u.t[:, 0:256], AF.Relu), R=[u], W=[rr])
                kb.op('dve', lambda e: e.tensor_tensor(hh.t[:], rr.t[:], rr.t[:], ALU.mult), R=[rr], W=[hh])
                for sub in range(2):
                    for half in range(2):
                        pa = pacc[sub * 2 + half]
                        kb.op('pe', lambda e: e.matmul(pa.t[:], hh.t[:, sub * 128:(sub + 1) * 128], w2b.t[:, f, half * 512:(half + 1) * 512], start=(f == 0), stop=(f == 31)),
                              R=[hh, w2b], W=[pa], acc=(f > 0))
            for sub in range(2):
                tok0 = t0 + sub * 128
                xx = x1[cnt % 2]; xq = xp[cnt % 2]; oo = ot[cnt % 2]
                cnt += 1
                kb.dma('sp', xx.t[:], x1d.t.ap()[tok0:tok0 + 128, :], R=[x1d], W=[xx])
                for half in range(2):
                    pa = pacc[sub * 2 + half]
                    kb.op('dve', lambda e: e.scalar_tensor_tensor(xq.t[:, half * 512:(half + 1) * 512], xx.t[:, half * 512:(half + 1) * 512], ALPHA, pa.t[:], ALU.mult, ALU.add),
                          R=[xx, pa], W=[xq])
                st, mv, rstd, nmr = (z[cnt % 2] for z in (st_, mv_, rstd_, nmr_))
                ln_stats(kb, xq, xq.t, st, mv, rstd, nmr, 1e-6)
                kb.op('act', lambda e: e.activation(oo.t[:], xq.t[:], AF.Identity, bias=nmr.t[:, 0:1], scale=rstd.t[:, 0:1]), R=[xq, nmr, rstd], W=[oo])
                kb.op('pool', lambda e: e.tensor_tensor(oo.t[:], oo.t[:], lnb.t[:, 0, :], ALU.mult), R=[oo, lnb], W=[oo])
                kb.op('pool', lambda e: e.tensor_tensor(oo.t[:], oo.t[:], lnb.t[:, 1, :], ALU.add), R=[oo, lnb], W=[oo])
                kb.dma('sp', out[tok0:tok0 + 128, :], oo.t[:], R=[oo], out=True)
    kb.barrier()
    stB.close()
    kb.stack = old


def _const_tables():
    t = np.arange(T)
    row = (t // 64).astype(np.float32)
    col = (t % 64).astype(np.float32)
    inv = (np.float32(10000.0) ** (-np.arange(16, dtype=np.float32) / np.float32(16))).astype(np.float32)
    cosd = np.zeros((128, T), np.float32)
    sind = np.zeros((128, T), np.float32)
    for p in range(128):
        d = p % 64
        pos = row if d < 32 else col
        i = d % 16
        ang = (pos * inv[i]).astype(np.float32)
        cosd[p] = np.cos(ang)
        sind[p] = np.sin(ang) * (-1.0 if (d % 32) < 16 else 1.0)
    return cosd, sind


def _bias_tables(rpb, heads):
    kk = np.arange(128)[:, None]
    qq = np.arange(128)[None, :]
    biasg = np.zeros((128, 5, 2, 640), np.float32)
    maskc = np.zeros((128, 5, 2, 640), np.float32)
    for v, g in enumerate((0, 1, 30, 62, 63)):
        m0 = min(max(g - 2, 0), 59)
        i = 2 * g + qq // 64
        cq = qq % 64
        r0 = np.clip(i - 4, 0, 120)
        c0 = np.clip(cq - 8, 0, 48)
        for j in range(5):
            kr = 2 * (m0 + j) + kk // 64
            kc = kk % 64
            valid = (kr >= r0) & (kr < r0 + 8) & (kc >= c0) & (kc < c0 + 16)
            ro = np.clip(kr - i + 7, 0, 14)
            co = np.clip(kc - cq + 15, 0, 30)
            for hd in range(2):
                biasg[:, v, hd, j * 128:(j + 1) * 128] = rpb[heads[hd]][ro, co]
                maskc[:, v, hd, j * 128:(j + 1) * 128] = np.where(valid, 0.0, NEG)
    return biasg.reshape(128, -1), maskc.reshape(128, -1)


def _swap_idx():
    d = np.arange(64)
    return np.where((d % 32) < 16, d + 16, d - 16)


def _prep(inputs):
    f = lambda k: np.asarray(inputs[k], dtype=np.float32)
    x, c, ctx, c_ctx = f('x'), f('c'), f('ctx'), f('c_ctx')
    w_in = f('w_in')[0]
    w_out = f('w_out')[0]
    rpb = f('na_rpb')[0]
    cosd, sind = _const_tables()
    sw = _swap_idx()
    maps = []
    for core in range(8):
        b, hg = core // 4, core % 4
        heads = (2 * hg, 2 * hg + 1)
        hc = np.concatenate([h * 64 + np.arange(64) for h in heads])
        hcs = np.concatenate([h * 64 + sw for h in heads])
        RW = 1536
        cols = np.concatenate([hc, hcs, 512 + hc, 512 + hcs, 1024 + hc,
                               RW + hc, RW + 512 + hc, RW + 1024 + hc,
                               RW + 1536 + np.arange(128), RW + 1664 + np.arange(128), RW + 1792 + np.arange(128)])
        biasg, maskc = _bias_tables(rpb, heads)
        m = {
            'xa': np.ascontiguousarray(np.concatenate([ctx[b], x[b]], 0)),
            'cct': np.ascontiguousarray(np.stack([c[b], c_ctx], 1)),
            'wmod': f('w_mod')[0],
            'bmod': np.ascontiguousarray(np.stack([f('b_mod')[0]] * 2, 1)),
            'wina': np.ascontiguousarray(w_in[:, cols]),
            'cosd': cosd, 'sind': sind, 'biasg': biasg, 'maskc': maskc,
            'wo': np.ascontiguousarray(np.concatenate([w_out[hg * 128:(hg + 1) * 128], w_out[512 + hg * 128:512 + (hg + 1) * 128]], 0)),
            'xb': np.ascontiguousarray(x[b, hg * 2048:(hg + 1) * 2048]),
            'w1': f('mlp_w1')[0], 'w2': f('mlp_w2')[0],
            'lnv': np.ascontiguousarray(np.stack([f('ln1_g')[0], f('ln1_b')[0], f('ln2_g')[0], f('ln2_b')[0]], 0)),
        }
        m.update(_prep_rw(inputs, b, hg))
        maps.append(m)
    return maps


def _prep_rw(inputs, b, hg):
    f = lambda k: np.asarray(inputs[k], dtype=np.float32)[0]
    heads = (2 * hg, 2 * hg + 1)
    hc = np.concatenate([h * 64 + np.arange(64) for h in heads])
    cols = np.concatenate([hc, 512 + hc, 1024 + hc, 1536 + np.arange(128), 1664 + np.arange(128), 1792 + np.arange(128)])
    rwv = np.zeros((4, 768), np.float32)
    rwv[0] = f('rw_mu_prev')[cols]
    rwv[1] = f('rw_mu_next')[cols]
    w0, a0 = f('rw_w0'), f('rw_a0')
    rwv[2, 0:128] = w0[0][hc]; rwv[2, 128:256] = w0[1][hc]
    rwv[2, 256:384] = a0[0][hc]; rwv[2, 384:512] = a0[1][hc]
    rwv[2, 512:640] = f('rw_k_k')[hc]
    rwv[3, 0:128] = f('rw_r_k').reshape(-1)[hc]
    rwv[3, 128:256] = f('rw_gn_g')[hc]
    rwv[3, 256:384] = f('rw_gn_b')[hc]
    rwv[3, 384:512] = f('rw_k_a')[hc]
    rwv[3, 512:640] = f('rw_k_a')[hc]
    rwv16 = np.zeros((16, 768), np.float32)
    rwv16[0:4] = rwv
    w2 = f('rw_w2')
    a2 = f('rw_a2')
    rww2 = np.zeros((128, 256), np.float32)
    rwa2 = np.zeros((128, 256), np.float32)
    for dd in range(2):
        rww2[dd * 64:(dd + 1) * 64, dd * 128:(dd + 1) * 128] = w2[dd][:, hc]
        rwa2[dd * 64:(dd + 1) * 64, dd * 128:(dd + 1) * 128] = a2[dd][:, hc]
    rwg2 = np.ascontiguousarray(f('rw_g2')[:, hc])
    i = np.arange(128)[:, None]
    j = np.arange(128)[None, :]
    tri = np.zeros((128, 768), np.float32)
    tri[:, 0:128] = (i < j)
    tri[:, 128:256] = (i > j)
    tri[:, 256:384] = (i <= j)
    tri[:, 384:512] = (i >= j)
    return {'rwv': rwv16, 'rww2': rww2, 'rwa2': rwa2, 'rwg2': rwg2, 'tri': tri}


_NC = None


def kernel(**inputs):
    global _NC
    if _NC is None:
        _NC = build_program()
    maps = _prep(inputs)
    res = run_bass_kernel_spmd(_NC, maps, core_ids=list(range(8)))
    out = np.zeros((2, T, D), np.float32)
    for core in range(8):
        b, q = core // 4, core % 4
        out[b, q * 2048:(q + 1) * 2048] = np.asarray(res.results[core]['out'])
    return out
```

```python
import numpy as np
from contextlib import ExitStack, contextmanager
import concourse.bass as bass
import concourse.mybir as mybir
from concourse.bass_utils import run_bass_kernel_spmd

F32 = mybir.dt.float32
BF16 = mybir.dt.bfloat16
AF = mybir.ActivationFunctionType
ALU = mybir.AluOpType

D = 1024
T = 8192
L = 256
NTOK = T + L
ALPHA = 2.0 ** 0.25
NEG = -30000.0
WCOLS = 1408
PRW_ROWS = NTOK + 4


class Buf:
    def __init__(self, t, name):
        self.t = t
        self.name = name
        self.w = None
        self.r = []
        self.psum = False


class KB:
    NDMA = 6

    def __init__(self, nc):
        self.nc = nc
        self.eng = {'pe': nc.tensor, 'act': nc.scalar, 'dve': nc.vector,
                    'pool': nc.gpsimd, 'sp': nc.sync}
        self.sem = {}
        self.cnt = {}
        for e in ['pe', 'act', 'dve', 'pool']:
            self._mksem(e)
        self.dq = {}
        for q in ['sp', 'pool']:
            names = ["dma_%s_%d" % (q, i) for i in range(self.NDMA)]
            for n in names:
                self._mksem(n)
            self.dq[q] = [names, 0]
        self.seen = {e: {} for e in self.eng}
        self.out_events = []
        self.uid = 0
        self.stack = None
        import os as _os
        self.relax = _os.environ.get('KDEV_RELAX', '0') == '1'

    def _mksem(self, name):
        self.sem[name] = self.nc.alloc_semaphore(name)
        self.cnt[name] = 0

    def sb(self, name, shape, dt):
        self.uid += 1
        nm = "%s_%d" % (name, self.uid)
        return Buf(self.stack.enter_context(self.nc.sbuf_tensor(nm, list(shape), dt)), nm)

    def ps(self, name, shape, dt=F32):
        self.uid += 1
        nm = "%s_%d" % (name, self.uid)
        b = Buf(self.stack.enter_context(self.nc.psum_tensor(nm, list(shape), dt)), nm)
        b.psum = True
        return b

    def dram(self, name, shape, dt):
        return Buf(self.nc.dram_tensor(name, list(shape), dt), name)

    @contextmanager
    def phase(self):
        old = self.stack
        with ExitStack() as st:
            self.stack = st
            yield
            self.barrier()
        self.stack = old

    def barrier(self):
        for e in ['pe', 'act', 'dve', 'pool', 'sp']:
            for s, v in self.cnt.items():
                if v > 0 and self.seen[e].get(s, 0) < v:
                    self.eng[e].wait_ge(self.sem[s], v)
                    self.seen[e][s] = v

    def _need(self, engine, ev, waits):
        if ev is None:
            return
        s, v = ev
        if engine == 'pe' and s == 'pe':
            return
        if self.relax and engine == s and engine in ('act', 'dve'):
            return
        if self.seen[engine].get(s, 0) >= v:
            return
        if waits.get(s, 0) < v:
            waits[s] = v

    def _emit_waits(self, engine, waits):
        e = self.eng[engine]
        for s, v in waits.items():
            e.wait_ge(self.sem[s], v)
            self.seen[engine][s] = v

    def _record(self, ev, R, W):
        for b in R:
            b.r.append(ev)
            if len(b.r) > 16:
                best = {}
                for s, v in b.r:
                    if best.get(s, 0) < v:
                        best[s] = v
                b.r = list(best.items())
        for b in W:
            b.w = ev
            b.r = []

    def op(self, engine, fn, R=(), W=(), acc=False):
        W = list(W) + [b for b in R if b.psum]
        R = [b for b in R if not b.psum]
        waits = {}
        for b in R:
            self._need(engine, b.w, waits)
        for b in W:
            if not acc:
                self._need(engine, b.w, waits)
                for ev in b.r:
                    self._need(engine, ev, waits)
        self._emit_waits(engine, waits)
        ins = fn(self.eng[engine])
        self.cnt[engine] += 1
        ins.then_inc(self.sem[engine], 1)
        self._record((engine, self.cnt[engine]), R, W)

    def dma(self, q, out_ap, in_ap, R=(), W=(), out=False, **kw):
        names, idx = self.dq[q]
        s = names[idx % len(names)]
        self.dq[q][1] = idx + 1
        waits = {}
        if self.cnt[s] > 0:
            self._need(q, (s, self.cnt[s]), waits)
        for b in R:
            self._need(q, b.w, waits)
        for b in W:
            self._need(q, b.w, waits)
            for ev in b.r:
                self._need(q, ev, waits)
        self._emit_waits(q, waits)
        ins = self.eng[q].dma_start(out=out_ap, in_=in_ap, **kw)
        self.cnt[s] += 16
        ins.then_inc(self.sem[s], 16)
        ev = (s, self.cnt[s])
        self._record(ev, R, W)
        if out:
            self.out_events.append(ev)
        return ev

    def finish(self):
        self.barrier()


def ln_stats(kb, xt, xap, st, mv, rstd, nmr, eps):
    kb.op('dve', lambda e: e.bn_stats(st.t[:, 0, :], xap[:, 0:512]), R=[xt], W=[st])
    kb.op('dve', lambda e: e.bn_stats(st.t[:, 1, :], xap[:, 512:1024]), R=[xt], W=[st])
    kb.op('dve', lambda e: e.bn_aggr(mv.t[:], st.t[:]), R=[st], W=[mv])
    kb.op('act', lambda e: e.activation(rstd.t[:, 0:1], mv.t[:, 1:2], AF.Sqrt, bias=eps, scale=1.0), R=[mv], W=[rstd])
    kb.op('dve', lambda e: e.reciprocal(rstd.t[:, 0:1], rstd.t[:, 0:1]), R=[rstd], W=[rstd])
    kb.op('dve', lambda e: e.scalar_tensor_tensor(nmr.t[:, 0:1], mv.t[:, 0:1], -1.0, rstd.t[:, 0:1], ALU.mult, ALU.mult),
          R=[mv, rstd], W=[nmr])


def build_program(stages=("mod", "inproj", "na", "rwkv", "oproj", "gather", "b1"), debug=False):
    nc = bass.Bass("TRN2", target_bir_lowering=False)

    def din(name, shape, dt=F32):
        return nc.dram_tensor(name, list(shape), dt, kind="ExternalInput").ap()

    A = {}
    A['xa'] = din("xa", [NTOK, D])
    A['cct'] = din("cct", [D, 2])
    A['wmod'] = din("wmod", [D, 6 * D])
    A['bmod'] = din("bmod", [6 * D, 2])
    A['wina'] = din("wina", [D, WCOLS])
    A['cosd'] = din("cosd", [128, T])
    A['sind'] = din("sind", [128, T])
    A['biasg'] = din("biasg", [128, 5 * 2 * 640])
    A['maskc'] = din("maskc", [128, 5 * 2 * 640])
    A['wo'] = din("wo", [256, D])
    A['xb'] = din("xb", [2048, D])
    A['w1'] = din("w1", [D, 4 * D])
    A['w2'] = din("w2", [4 * D, D])
    A['lnv'] = din("lnv", [4, D])
    A['rwv'] = din("rwv", [16, 768])
    A['rww2'] = din("rww2", [128, 256])
    A['rwa2'] = din("rwa2", [128, 256])
    A['rwg2'] = din("rwg2", [128, 128])
    A['tri'] = din("tri", [128, 6 * 128])
    out = nc.dram_tensor("out", [2048, D], F32, kind="ExternalOutput").ap()
    dbg = {}
    if debug:
        dbg['ycat'] = nc.dram_tensor("dbg_ycat", [T, 256], BF16, kind="ExternalOutput").ap()
        dbg['modT'] = nc.dram_tensor("dbg_modT", [128, 96], F32, kind="ExternalOutput").ap()
        dbg['prw'] = nc.dram_tensor("dbg_prw", [PRW_ROWS, 768], F32, kind="ExternalOutput").ap()
        dbg['x1'] = nc.dram_tensor("dbg_x1", [2048, D], F32, kind="ExternalOutput").ap()
        dbg['yrw_in'] = din("dbg_yrw", [T, 128], BF16)

    kb = KB(nc)
    prw = kb.dram("prw", [PRW_ROWS, 768], F32)
    ycat = kb.dram("ycat", [T, 256], BF16)
    opart = kb.dram("opart", [T, D], F32)
    osum = kb.dram("osum", [2048, D], F32)
    x1d = kb.dram("x1d", [2048, D], F32)

    with ExitStack() as outer:
        kb.stack = outer
        modT = kb.sb("modT", [128, 48, 2], F32)
        modC = kb.sb("modC", [128, 48, 2], F32)
        ident = kb.sb("ident", [128, 128], BF16)
        identf = kb.sb("identf", [128, 128], F32)
        onesf = kb.sb("onesf", [128, 128], F32)
        kb.op('pool', lambda e: e.memset(identf.t[:], 0.0), W=[identf])
        kb.op('pool', lambda e: e.affine_select(identf.t[:], identf.t[:], pattern=[[-1, 128]],
                                                compare_op=ALU.not_equal, fill=1.0, base=0,
                                                channel_multiplier=1), R=[identf], W=[identf])
        kb.op('dve', lambda e: e.tensor_copy(ident.t[:], identf.t[:]), R=[identf], W=[ident])
        kb.op('pool', lambda e: e.memset(onesf.t[:], 1.0), W=[onesf])

        if "mod" in stages:
            with kb.phase():
                cT = kb.sb("cT", [128, 8, 2], F32)
                scT = kb.sb("scT", [128, 8, 2], F32)
                bT = kb.sb("bT", [128, 48, 2], F32)
                kb.dma('sp', cT.t[:], A['cct'].rearrange("(k p) r -> p k r", p=128), W=[cT])
                kb.dma('sp', bT.t[:], A['bmod'].rearrange("(j p) r -> p j r", p=128), W=[bT])
                kb.op('act', lambda e: e.activation(scT.t[:], cT.t[:], AF.Silu), R=[cT], W=[scT])
                wm = [kb.sb("wm%d" % i, [128, 8, 512], F32) for i in range(4)]
                pm = [kb.ps("pm%d" % i, [128, 512], F32) for i in range(2)]
                for jg in range(12):
                    w = wm[jg % 4]
                    ps = pm[jg % 2]
                    kb.dma('sp', w.t[:, 0:4, :],
                           A['wmod'][0:512, jg * 512:(jg + 1) * 512].rearrange("(k p) n -> p k n", p=128), W=[w])
                    kb.dma('pool', w.t[:, 4:8, :],
                           A['wmod'][512:1024, jg * 512:(jg + 1) * 512].rearrange("(k p) n -> p k n", p=128), W=[w])
                    for jj in range(4):
                        for k in range(8):
                            kb.op('pe', lambda e: e.matmul(ps.t[:, jj * 2:(jj + 1) * 2],
                                                           w.t[:, k, jj * 128:(jj + 1) * 128],
                                                           scT.t[:, k, :], start=(k == 0), stop=(k == 7)),
                                  R=[w, scT], W=[ps], acc=(k > 0))
                    kb.op('dve', lambda e: e.tensor_tensor(
                        modT.t[:, jg * 4:(jg + 1) * 4, :],
                        ps.t[:, 0:8].rearrange("p (j r) -> p j r", r=2),
                        bT.t[:, jg * 4:(jg + 1) * 4, :], ALU.add), R=[ps, bT], W=[modT])
                kb.op('dve', lambda e: e.tensor_scalar_add(modT.t[:, 8:16, :], modT.t[:, 8:16, :], 1.0), R=[modT], W=[modT])
                kb.op('dve', lambda e: e.tensor_scalar_add(modT.t[:, 32:40, :], modT.t[:, 32:40, :], 1.0), R=[modT], W=[modT])
                kb.op('dve', lambda e: e.tensor_copy(modC.t[:], modT.t[:]), R=[modT], W=[modC])
                kb.op('dve', lambda e: e.tensor_copy(modC.t[:, :, 0:1], modT.t[:, :, 1:2]), R=[modT], W=[modC])
                if debug:
                    kb.dma('sp', dbg['modT'], modT.t[:].rearrange("p j r -> p (j r)"), R=[modT], out=True)

        if "inproj" in stages:
            stA = ExitStack()
            outerA = kb.stack
            kb.stack = stA
            QrT = kb.sb("QrT", [128, T], BF16)
            QpT = kb.sb("QpT", [128, T], BF16)
            KrT = kb.sb("KrT", [128, T], BF16)
            KcT = kb.sb("KcT", [128, L], BF16)
            Vaug = kb.sb("Vaug", [128, 66, 2, 80], BF16)
            kb.op('pool', lambda e: e.memset(Vaug.t[:], 1.0), W=[Vaug])
            with kb.phase():
                wb = kb.sb("winb", [128, 8, WCOLS], BF16)
                wst = [kb.sb("wst%d" % i, [128, 2, WCOLS], F32) for i in range(2)]
                for kk in range(4):
                    w = wst[kk % 2]
                    kb.dma('pool', w.t[:], A['wina'][kk * 256:(kk + 1) * 256, :].rearrange("(k p) n -> p k n", p=128), W=[w])
                    kb.op('pool', lambda e: e.tensor_copy(wb.t[:, kk * 2:(kk + 1) * 2, :], w.t[:]), R=[w], W=[wb])
                zrow = kb.sb("zrow", [1, 768], F32)
                kb.op('pool', lambda e: e.memset(zrow.t[:], 0.0), W=[zrow])
                for rr in (0, 257, 258, PRW_ROWS - 1):
                    kb.dma('pool', prw.t.ap()[rr:rr + 1, :], zrow.t[:], R=[zrow], W=[prw])
                xt = [kb.sb("xt%d" % i, [128, D], F32) for i in range(2)]
                xn = [kb.sb("xn%d" % i, [128, D], BF16) for i in range(2)]
                st_ = [kb.sb("st%d" % i, [128, 2, 6], F32) for i in range(2)]
                mv_ = [kb.sb("mv%d" % i, [128, 2], F32) for i in range(2)]
                rstd_ = [kb.sb("rstd%d" % i, [128, 2], F32) for i in range(2)]
                nmr_ = [kb.sb("nmr%d" % i, [128, 2], F32) for i in range(2)]
                hT = [kb.sb("hT%d" % i, [128, 8, 512], BF16) for i in range(2)]
                ptr = [kb.ps("ptr%d" % i, [128, 1024], BF16) for i in range(2)]
                pf = [kb.ps("pf%d" % i, [128, 512], F32) for i in range(4)]
                pt_ = [kb.ps("ptk%d" % i, [128, 512], F32) for i in range(2)]
                cs = kb.sb("cs", [128, 512], F32)
                sn = kb.sb("sn", [128, 512], F32)
                t1 = [kb.sb("t1_%d" % i, [128, 512], F32) for i in range(2)]
                t2 = [kb.sb("t2_%d" % i, [128, 512], F32) for i in range(2)]
                rwsb = [kb.sb("rwsb%d" % i, [128, 768], F32) for i in range(2)]
                nsubtot = 0
                import os as _os
                for bi in range(int(_os.environ.get('KDEV_NBLK', '17'))):
                    isctx = (bi == 0)
                    nsub = 2 if isctx else 4
                    g0 = 0 if isctx else L + (bi - 1) * 512
                    ntok = nsub * 128
                    mm = modC if isctx else modT
                    h = hT[bi % 2]
                    for sub in range(nsub):
                        xx = xt[nsubtot % 2]
                        xb = xn[nsubtot % 2]
                        pp = ptr[nsubtot % 2]
                        kb.dma('sp', xx.t[:], A['xa'][g0 + sub * 128: g0 + (sub + 1) * 128, :], W=[xx])
                        st, mv, rstd, nmr = (z[nsubtot % 2] for z in (st_, mv_, rstd_, nmr_))
                        ln_stats(kb, xx, xx.t, st, mv, rstd, nmr, 1e-6)
                        kb.op('act', lambda e: e.activation(xb.t[:], xx.t[:], AF.Identity, bias=nmr.t[:, 0:1], scale=rstd.t[:, 0:1]),
                              R=[xx, nmr, rstd], W=[xb])
                        _pp = float(_os.environ.get('KDEV_PART', '9'))
                        for k in range(8 if _pp >= 0.5 else 0):
                            kb.op('pe', lambda e: e.transpose(pp.t[:, k * 128:(k + 1) * 128], xb.t[:, k * 128:(k + 1) * 128], ident.t[:]),
                                  R=[xb, ident], W=[pp], acc=(k > 0))
                        for k in range(8 if _pp >= 0.7 else 0):
                            kb.op('act', lambda e: e.activation(h.t[:, k, sub * 128:(sub + 1) * 128], pp.t[:, k * 128:(k + 1) * 128],
                                                                AF.Identity, bias=mm.t[:, k, 0:1], scale=mm.t[:, 8 + k, 0:1]),
                                  R=[pp, mm], W=[h])
                        nsubtot += 1
                    if float(_os.environ.get('KDEV_PART', '9')) < 2:
                        continue
                    if isctx:
                        p = pf[2]
                        for k in range(8):
                            kb.op('pe', lambda e: e.matmul(p.t[:, 0:ntok], wb.t[:, k, 256:384], h.t[:, k, 0:ntok], start=(k == 0), stop=(k == 7)),
                                  R=[wb, h], W=[p], acc=(k > 0))
                        kb.op('act', lambda e: e.copy(KcT.t[:, 0:ntok], p.t[:, 0:ntok]), R=[p], W=[KcT])
                    else:
                        l0 = g0 - L
                        kb.dma('sp', cs.t[:], A['cosd'][:, l0:l0 + 512], W=[cs])
                        kb.dma('sp', sn.t[:], A['sind'][:, l0:l0 + 512], W=[sn])
                        for og in range(4):
                            p = pf[og]
                            for k in range(8):
                                kb.op('pe', lambda e: e.matmul(p.t[:], wb.t[:, k, og * 128:(og + 1) * 128], h.t[:, k, :], start=(k == 0), stop=(k == 7)),
                                      R=[wb, h], W=[p], acc=(k > 0))
                        _sk = _os.environ.get('KDEV_SKIP', '')
                        if 'q' not in _sk:
                            kb.op('act', lambda e: e.copy(QpT.t[:, l0:l0 + 512], pf[0].t[:]), R=[pf[0]], W=[QpT])
                        for qi, dst in (((0, QrT), (1, KrT)) if 'r' not in _sk else ()):
                            a1, a2 = t1[qi], t2[qi]
                            kb.op('dve', lambda e: e.tensor_tensor(a1.t[:], pf[2 * qi].t[:], cs.t[:], ALU.mult), R=[pf[2 * qi], cs], W=[a1])
                            kb.op('dve', lambda e: e.tensor_tensor(a2.t[:], pf[2 * qi + 1].t[:], sn.t[:], ALU.mult), R=[pf[2 * qi + 1], sn], W=[a2])
                            if 'p' not in _sk:
                                kb.op('pool', lambda e: e.tensor_tensor(dst.t[:, l0:l0 + 512], a1.t[:], a2.t[:], ALU.add), R=[a1, a2], W=[dst])
                    if float(_os.environ.get('KDEV_PART', '9')) < 3:
                        continue
                    for sub in range(nsub):
                        pa, pb = pt_[0], pt_[1]
                        for k in range(8):
                            kb.op('pe', lambda e: e.matmul(pa.t[:], h.t[:, k, sub * 128:(sub + 1) * 128], wb.t[:, k, 512:1024], start=(k == 0), stop=(k == 7)),
                                  R=[wb, h], W=[pa], acc=(k > 0))
                        for k in range(8):
                            kb.op('pe', lambda e: e.matmul(pb.t[:, 0:384], h.t[:, k, sub * 128:(sub + 1) * 128], wb.t[:, k, 1024:1408], start=(k == 0), stop=(k == 7)),
                                  R=[wb, h], W=[pb], acc=(k > 0))
                        tile = (g0 // 128) + sub
                        kb.op('act', lambda e: e.copy(Vaug.t[:, tile, :, 0:64], pa.t[:, 0:128].rearrange("p (h d) -> p h d", h=2)), R=[pa], W=[Vaug])
                        rw = rwsb[sub % 2]
                        kb.op('dve', lambda e: e.tensor_copy(rw.t[:, 0:384], pa.t[:, 128:512]), R=[pa], W=[rw])
                        kb.op('act', lambda e: e.copy(rw.t[:, 384:768], pb.t[:, 0:384]), R=[pb], W=[rw])
                        row0 = (1 if isctx else 259 - L) + g0 + sub * 128
                        kb.dma('pool', prw.t.ap()[row0:row0 + 128, :], rw.t[:], R=[rw], W=[prw])
            if debug:
                for r0 in range(0, PRW_ROWS, 1024):
                    r1 = min(PRW_ROWS, r0 + 1024)
                    kb.dma('sp', dbg['prw'][r0:r1, :], prw.t.ap()[r0:r1, :], R=[prw], out=True)

        if "na" in stages:
            with kb.phase():
                bias = kb.sb("bias", [128, 10, 640], F32)
                btmp = kb.sb("btmp", [128, 10, 640], F32)
                kb.dma('sp', bias.t[:], A['biasg'].rearrange("p (v n) -> p v n", v=10), W=[bias])
                kb.dma('pool', btmp.t[:], A['maskc'].rearrange("p (v n) -> p v n", v=10), W=[btmp])
                kb.op('pool', lambda e: e.tensor_tensor(bias.t[:], bias.t[:], btmp.t[:], ALU.add), R=[bias, btmp], W=[bias])
                psA = [kb.ps("psA%d" % i, [128, 512], F32) for i in range(2)]
                psB = [kb.ps("psB%d" % i, [128, 512], F32) for i in range(2)]
                pso = [kb.ps("pso%d" % i, [128, 512], F32) for i in range(2)]
                sA = [kb.sb("sA%d" % i, [128, 640], F32) for i in range(2)]
                PT = [kb.sb("PT%d" % i, [128, 896], BF16) for i in range(2)]
                rec = [kb.sb("rec%d" % i, [128, 1], F32) for i in range(2)]
                yb = [kb.sb("yb%d" % i, [128, 8, 128], BF16) for i in range(2)]
                it = 0
                for g in range(64):
                    m0 = min(max(g - 2, 0), 59)
                    var = {0: 0, 1: 1, 62: 3, 63: 4}.get(g, 2)
                    ybuf = yb[(g // 8) % 2]
                    for hd in range(2):
                        hp = slice(hd * 64, (hd + 1) * 64)
                        pA, pB, po = psA[it % 2], psB[it % 2], pso[it % 2]
                        s, P, rc = sA[it % 2], PT[it % 2], rec[it % 2]
                        it += 1
                        qs = slice(g * 128, (g + 1) * 128)
                        for j in range(5):
                            dst = pA.t[:, j * 128:(j + 1) * 128] if j < 4 else pB.t[:, 0:128]
                            kb.op('pe', lambda e: e.matmul(dst, KrT.t[hp, (m0 + j) * 128:(m0 + j + 1) * 128], QrT.t[hp, qs], start=True, stop=True),
                                  R=[KrT, QrT], W=[pA if j < 4 else pB], acc=(j not in (0, 4)))
                        for j in range(2):
                            kb.op('pe', lambda e: e.matmul(pB.t[:, (j + 1) * 128:(j + 2) * 128], KcT.t[hp, j * 128:(j + 1) * 128], QpT.t[hp, qs], start=True, stop=True),
                                  R=[KcT, QpT], W=[pB], acc=True)
                        bv = var * 2 + hd
                        kb.op('dve', lambda e: e.scalar_tensor_tensor(s.t[:, 0:512], pA.t[:], 0.125, bias.t[:, bv, 0:512], ALU.mult, ALU.add),
                              R=[pA, bias], W=[s])
                        kb.op('dve', lambda e: e.scalar_tensor_tensor(s.t[:, 512:640], pB.t[:, 0:128], 0.125, bias.t[:, bv, 512:640], ALU.mult, ALU.add),
                              R=[pB, bias], W=[s])
                        kb.op('act', lambda e: e.activation(P.t[:, 0:640], s.t[:], AF.Exp), R=[s], W=[P])
                        kb.op('act', lambda e: e.activation(P.t[:, 640:896], pB.t[:, 128:384], AF.Exp, scale=0.125), R=[pB], W=[P])
                        for j in range(7):
                            vt = (2 + m0 + j) if j < 5 else (j - 5)
                            kb.op('pe', lambda e: e.matmul(po.t[:, 0:65], P.t[:, j * 128:(j + 1) * 128], Vaug.t[:, vt, hd, 0:65], start=(j == 0), stop=(j == 6)),
                                  R=[P, Vaug], W=[po], acc=(j > 0))
                        kb.op('dve', lambda e: e.reciprocal(rc.t[:], po.t[:, 64:65]), R=[po], W=[rc])
                        kb.op('act', lambda e: e.activation(ybuf.t[:, g % 8, hd * 64:(hd + 1) * 64], po.t[:, 0:64], AF.Copy, scale=rc.t[:]),
                              R=[po, rc], W=[ybuf])
                    if g % 8 == 7:
                        g8 = g - 7
                        kb.dma('sp', ycat.t.ap()[g8 * 128:(g8 + 8) * 128, 0:128].rearrange("(j p) c -> p j c", p=128), ybuf.t[:], R=[ybuf], W=[ycat])

        if "inproj" in stages:
            kb.barrier()
            stA.close()
            kb.stack = outerA

        if "rwkv" in stages:
            stage_rwkv(kb, A, prw, ycat, identf, onesf)
        if debug:
            for r0 in range(0, T, 2048):
                kb.dma('sp', dbg['ycat'][r0:r0 + 2048, :], ycat.t.ap()[r0:r0 + 2048, :], R=[ycat], out=True)

        if debug and "rwkv" not in stages:
            for r0 in range(0, T, 2048):
                kb.dma('sp', ycat.t.ap()[r0:r0 + 2048, 128:256], dbg['yrw_in'][r0:r0 + 2048, :], W=[ycat])
        if "oproj" in stages:
            stage_oproj(kb, A, ycat, opart, modT, ident, identf, onesf)
        if "gather" in stages:
            kb.barrier()
            ccs = nc.alloc_semaphore("ccs")
            nc.gpsimd.collective_compute("ReduceScatter", ALU.add, replica_groups=[[0, 1, 2, 3], [4, 5, 6, 7]],
                                         ins=[opart.t.ap().opt()], outs=[osum.t.ap().opt()]).then_inc(ccs)
            pre = stage_b_pre(kb, A, modT, identf, onesf) if "b1" in stages else None
            for e in ['pe', 'act', 'dve', 'pool', 'sp']:
                kb.eng[e].wait_ge(ccs, 1)

        if "b1" in stages:
            stage_b(kb, A, osum, x1d, out, modT, ident, identf, onesf, dbg, nc, pre)
        kb.finish()
    return nc


def stage_rwkv(kb, A, prw, ycat, identf, onesf):
    CW = -0.6065306597126334
    import os as _os
    NLAT = int(_os.environ.get('KDEV_NLAT', '64'))
    st0 = ExitStack()
    old = kb.stack
    kb.stack = st0
    op = kb.op
    AXX = mybir.AxisListType.X
    rwv = kb.sb("rwv", [128, 4, 768], F32)
    for i in range(4):
        kb.dma('sp', rwv.t[:, i, :], A['rwv'][i:i + 1, :].partition_broadcast(128), W=[rwv])
    MUP = rwv.t[:, 0, :]
    MUN = rwv.t[:, 1, :]
    KKB = rwv.t[:, 2, 512:640]
    RKB = rwv.t[:, 3, 0:128]
    GNG = rwv.t[:, 3, 128:256]
    GNB = rwv.t[:, 3, 256:384]
    KA2 = rwv.t[:, 3, 384:640]
    onem = kb.sb("onem", [128, 768], F32)
    omk2 = kb.sb("omk2", [128, 256], F32)
    op('dve', lambda e: e.tensor_tensor(onem.t[:], MUP, MUN, ALU.add), R=[rwv], W=[onem])
    op('dve', lambda e: e.tensor_scalar(onem.t[:], onem.t[:], -1.0, 1.0, ALU.mult, ALU.add), R=[onem], W=[onem])
    op('dve', lambda e: e.tensor_scalar(omk2.t[:], KA2, -1.0, 1.0, ALU.mult, ALU.add), R=[rwv], W=[omk2])
    w2s = kb.sb("w2s", [128, 640], F32)
    kb.dma('sp', w2s.t[:, 0:256], A['rww2'], W=[w2s])
    kb.dma('sp', w2s.t[:, 256:512], A['rwa2'], W=[w2s])
    kb.dma('sp', w2s.t[:, 512:640], A['rwg2'], W=[w2s])
    tri = kb.sb("tri", [128, 4, 128], F32)
    kb.dma('sp', tri.t[:], A['tri'][:, 0:512].rearrange("p (m n) -> p m n", m=4), W=[tri])
    SU, SL, IU, IL = 0, 1, 2, 3
    mb = [kb.sb("mb%d" % d, [128, 5, 128], F32) for d in range(2)]
    for d, order in enumerate(((SU, IU, SU, IU, SL), (SL, IL, SL, IL, SU))):
        for i, m in enumerate(order):
            op('pool', lambda e: e.tensor_copy(mb[d].t[:, i, :], tri.t[:, m, :]), R=[tri], W=[mb[d]])
    Yst = [kb.sb("Yst%d" % d, [128, 64, 128], F32) for d in range(2)]

    class Lane:
        pass

    identr = kb.sb("identr", [128, 128], mybir.dt.float32r)
    op('pool', lambda e: e.tensor_copy(identr.t[:], identf.t[:]), R=[identf], W=[identr])
    zt = kb.sb("zt", [128, 512], F32)
    op('pool', lambda e: e.memset(zt.t[:], 0.0), W=[zt])

    lanes = []
    for d in range(2):
        ln = Lane()
        n = lambda nm: "%s_l%d" % (nm, d)
        ln.BF, ln.BA, ln.BB, ln.BC = [kb.ps(n("rp%d" % i), [128, 512], F32) for i in range(4)]
        ln.p0 = kb.sb(n("p0"), [128, 768], F32)
        ln.pm = kb.sb(n("pm"), [128, 768], F32)
        ln.pp = kb.sb(n("pp"), [128, 768], F32)
        ln.ps2 = [kb.sb(n("ps%d" % i), [128, 768], F32) for i in range(2)]
        ln.act3 = kb.sb(n("act3"), [128, 384], F32)
        ln.act3T = kb.sb(n("act3T"), [128, 384], F32)
        ln.tmpw = kb.sb(n("tmpw"), [128, 512], F32)
        ln.lw = kb.sb(n("lw"), [128, 256], F32)
        ln.aa = kb.sb(n("aa"), [128, 256], F32)
        ln.gg2 = [kb.sb(n("gg%d" % i), [128, 128], F32) for i in range(2)]
        ln.kq = kb.sb(n("kq"), [128, 128], F32)
        ln.sq = kb.sb(n("sq"), [128, 128], F32)
        ln.ss = kb.sb(n("ss"), [128, 2, 2], F32)
        ln.ss2 = kb.sb(n("ss2"), [128, 2], F32)
        ln.kk = kb.sb(n("kk"), [128, 128], F32)
        ln.kd = kb.sb(n("kd"), [128, 256], F32)
        ln.tk = kb.sb(n("tk"), [128, 256], F32)
        ln.rr = kb.sb(n("rr"), [128, 128], F32)
        ln.bs4 = kb.sb(n("bs4"), [128, 4], F32)
        ln.bsum2 = [kb.sb(n("bsum%d" % i), [128, 2, 2], F32) for i in range(2)]
        ln.lgs = kb.sb(n("lgs"), [128, 256], F32)
        ln.E = kb.sb(n("E"), [128, 4, 128], F32)
        ln.akk = kb.sb(n("akk"), [128, 128], F32)
        ln.Q62 = [kb.sb(n("Q6_%d" % i), [128, 6, 128], F32) for i in range(2)]
        ln.QT = kb.sb(n("QT"), [64, 2, 4, 128], mybir.dt.float32r)
        ln.gcol2 = [kb.sb(n("gcol%d" % i), [64, 2, 2], F32) for i in range(2)]
        ln.G = kb.sb(n("G"), [128, 2, 3, 128], F32)
        ln.Gk = kb.sb(n("Gk"), [128, 2, 128], F32)
        FRD = mybir.dt.float32r if _os.environ.get('KDEV_F32R', '1') == '1' else F32
        ln.C = [[kb.sb(n("C%d_%d" % (i, h)), [128, 4, 128], FRD) for h in range(2)] for i in range(2)]
        for i in range(2):
            for h in range(2):
                op('pool', lambda e: e.tensor_copy(ln.C[i][h].t[:].rearrange("p a t -> p (a t)"), zt.t[:]), R=[zt], W=[ln.C[i][h]])
        ln.Xs = kb.sb(n("Xs"), [128, 128], F32)
        ln.Us = kb.sb(n("Us"), [128, 128], F32)
        ln.yv = kb.sb(n("yv"), [128, 128], F32)
        ln.gst = kb.sb(n("gst"), [128, 2, 6], F32)
        ln.gmv = kb.sb(n("gmv"), [128, 2, 2], F32)
        ln.grs = kb.sb(n("grs"), [128, 2, 2], F32)
        ln.gnm = kb.sb(n("gnm"), [128, 2, 2], F32)
        ln.yo = [kb.sb(n("yo%d" % i), [128, 128], BF16) for i in range(2)]
        ln.Z = kb.sb(n("Z"), [64, 128], F32)
        ln.nout = 0
        lanes.append(ln)

    def unpack(ln, vis):
        row0, d, is_ctx, chunk, readout, par = vis
        return dict(p0=ln.p0, pm=ln.pm, pp=ln.pp, ps=ln.ps2[par], act3=ln.act3, act3T=ln.act3T, tmpw=ln.tmpw, lw=ln.lw, aa=ln.aa,
                    gg=ln.gg2[par], kq=ln.kq, sq=ln.sq, ss=ln.ss, ss2=ln.ss2, kk=ln.kk, kd=ln.kd, tk=ln.tk, rr=ln.rr, bs4=ln.bs4,
                    bsum=ln.bsum2[par], lgs=ln.lgs, E=ln.E, akk=ln.akk, Q6=ln.Q62[par], QT=ln.QT, gcol=ln.gcol2[par], G=ln.G, Gk=ln.Gk,
                    Xs=ln.Xs, Us=ln.Us, yv=ln.yv, Z=ln.Z)

    def front(ln, vis):
        row0, d, is_ctx, chunk, readout, par = vis
        t_ = unpack(ln, vis)
        p0, pm, pp, ps, act3, act3T, tmpw, lw, aa, gg = (t_[k] for k in ('p0', 'pm', 'pp', 'ps', 'act3', 'act3T', 'tmpw', 'lw', 'aa', 'gg'))
        kq, sq, ss, ss2, kk, kd, tk, rr = (t_[k] for k in ('kq', 'sq', 'ss', 'ss2', 'kk', 'kd', 'tk', 'rr'))
        bs4, bsum, lgs, E, akk, Q6, gcol = (t_[k] for k in ('bs4', 'bsum', 'lgs', 'E', 'akk', 'Q6', 'gcol'))
        B0 = B1 = B2 = ln.BF
        dq = 'sp' if d == 0 else 'pool'
        kb.dma(dq, p0.t[:], prw.t.ap()[row0:row0 + 128, :], R=[prw], W=[p0])
        kb.dma(dq, pm.t[:], prw.t.ap()[row0 - 1:row0 + 127, :], R=[prw], W=[pm])
        kb.dma(dq, pp.t[:], prw.t.ap()[row0 + 1:row0 + 129, :], R=[prw], W=[pp])
        yield
        op('dve', lambda e: e.tensor_tensor(ps.t[:], p0.t[:], onem.t[:], ALU.mult), R=[p0, onem], W=[ps])
        op('pool', lambda e: e.tensor_tensor(pm.t[:], pm.t[:], MUP, ALU.mult), R=[pm, rwv], W=[pm])
        op('pool', lambda e: e.tensor_tensor(pp.t[:], pp.t[:], MUN, ALU.mult), R=[pp, rwv], W=[pp])
        yield
        op('dve', lambda e: e.tensor_tensor(ps.t[:], ps.t[:], pm.t[:], ALU.add), R=[ps, pm], W=[ps])
        op('dve', lambda e: e.tensor_tensor(ps.t[:], ps.t[:], pp.t[:], ALU.add), R=[ps, pp], W=[ps])
        yield
        r_ = ps.t[:, 0:128]
        k_ = ps.t[:, 128:256]
        vh = lambda h: ps.t[:, 256 + h * 64:256 + (h + 1) * 64]
        op('act', lambda e: e.activation(act3.t[:, 0:128], ps.t[:, 384:512], AF.Tanh), R=[ps], W=[act3])
        op('act', lambda e: e.activation(act3.t[:, 256:384], ps.t[:, 640:768], AF.Sigmoid), R=[ps], W=[act3])
        op('pool', lambda e: e.tensor_copy(act3.t[:, 128:256], ps.t[:, 512:640]), R=[ps], W=[act3])
        yield
        for i in range(3):
            op('pe', lambda e: e.transpose(B0.t[:, i * 128:(i + 1) * 128], act3.t[:, i * 128:(i + 1) * 128], identf.t[:]),
               R=[act3, identf], W=[B0], acc=(i > 0))
        yield
        op('act', lambda e: e.copy(act3T.t[:], B0.t[:, 0:384]), R=[B0], W=[act3T])
        yield
        for i in range(2):
            op('pe', lambda e: e.matmul(B1.t[:, i * 256:(i + 1) * 256], act3T.t[:, i * 128:(i + 1) * 128],
                                        w2s.t[:, i * 256:(i + 1) * 256], start=True, stop=True),
               R=[act3T, w2s], W=[B1], acc=(i > 0))
        yield
        op('dve', lambda e: e.tensor_tensor(tmpw.t[:], B1.t[:], rwv.t[:, 2, 0:512], ALU.add), R=[B1, rwv], W=[tmpw])
        yield
        op('act', lambda e: e.activation(tmpw.t[:], tmpw.t[:], AF.Sigmoid), R=[tmpw], W=[tmpw])
        yield
        op('dve', lambda e: e.tensor_scalar_mul(lw.t[:], tmpw.t[:, 0:256], CW), R=[tmpw], W=[lw])
        op('pool', lambda e: e.tensor_copy(aa.t[:], tmpw.t[:, 256:512]), R=[tmpw], W=[aa])
        yield
        lwd = lw.t[:, d * 128:(d + 1) * 128]
        tmat = tri.t[:, IU if d == 0 else IL, :]
        op('pe', lambda e: e.matmul(B2.t[:, 128:256], tmat, lwd, start=True, stop=True), R=[tri, lw], W=[B2])
        op('pe', lambda e: e.matmul(B2.t[:, 256:384], onesf.t[:], lwd, start=True, stop=True), R=[onesf, lw], W=[B2], acc=True)
        for h in range(2):
            op('pe', lambda e: e.matmul(B0.t[0:64, 384 + 2 * h:385 + 2 * h], lw.t[:, d * 128 + h * 64: d * 128 + (h + 1) * 64], onesf.t[:, 0:1],
                                        start=True, stop=True), R=[lw, onesf], W=[B0], acc=True)
        if readout:
            op('pe', lambda e: e.matmul(B2.t[:, 0:128], act3T.t[:, 256:384], w2s.t[:, 512:640], start=True, stop=True),
               R=[act3T, w2s], W=[B2], acc=True)
        yield
        if readout:
            op('act', lambda e: e.copy(gg.t[:], B2.t[:, 0:128]), R=[B2], W=[gg])
        op('dve', lambda e: e.tensor_tensor(kq.t[:], k_, KKB, ALU.mult), R=[ps, rwv], W=[kq])
        op('pool', lambda e: e.tensor_tensor(tk.t[:], aa.t[:], KA2, ALU.mult), R=[aa, rwv], W=[tk])
        yield
        op('pool', lambda e: e.tensor_tensor(sq.t[:], kq.t[:], kq.t[:], ALU.mult), R=[kq], W=[sq])
        op('pool', lambda e: e.tensor_tensor(tk.t[:], tk.t[:], omk2.t[:], ALU.add), R=[tk, omk2], W=[tk])
        yield
        op('dve', lambda e: e.tensor_reduce(ss2.t[:], sq.t[:].rearrange("p (g j) -> p g j", j=64), AXX, ALU.add), R=[sq], W=[ss2])
        for dd in range(2):
            op('pool', lambda e: e.tensor_tensor(kd.t[:, dd * 128:(dd + 1) * 128], tk.t[:, dd * 128:(dd + 1) * 128], k_, ALU.mult),
               R=[tk, ps], W=[kd])
        yield
        op('act', lambda e: e.activation(ss2.t[:], ss2.t[:], AF.Sqrt), R=[ss2], W=[ss2])
        yield
        op('dve', lambda e: e.tensor_scalar_max(ss2.t[:], ss2.t[:], 1e-12), R=[ss2], W=[ss2])
        op('dve', lambda e: e.reciprocal(ss2.t[:], ss2.t[:]), R=[ss2], W=[ss2])
        op('dve', lambda e: e.tensor_copy(ss.t[:, :, 0:1], ss2.t[:].rearrange("p (h o) -> p h o", o=1)), R=[ss2], W=[ss])
        yield
        for h in range(2):
            op('dve', lambda e: e.tensor_scalar_mul(kk.t[:, h * 64:(h + 1) * 64], kq.t[:, h * 64:(h + 1) * 64], ss.t[:, h, 0:1]),
               R=[kq, ss], W=[kk])
        yield
        if readout:
            op('pool', lambda e: e.tensor_tensor(rr.t[:], r_, RKB, ALU.mult), R=[ps, rwv], W=[rr])
            for dd in range(2):
                op('pool', lambda e: e.tensor_tensor(tk.t[:, dd * 128:(dd + 1) * 128], kd.t[:, dd * 128:(dd + 1) * 128], rr.t[:], ALU.mult),
                   R=[kd, rr], W=[tk])
            yield
            op('dve', lambda e: e.tensor_reduce(bs4.t[:], tk.t[:].rearrange("p (g j) -> p g j", j=64), AXX, ALU.add),
               R=[tk], W=[bs4])
            op('dve', lambda e: e.tensor_tensor(bsum.t[:, :, 0:1], bs4.t[:, 0:2].rearrange("p (h o) -> p h o", o=1),
                                                bs4.t[:, 2:4].rearrange("p (h o) -> p h o", o=1), ALU.add), R=[bs4], W=[bsum])
            yield
        op('act', lambda e: e.activation(gcol.t[:, :, 0:1], B0.t[0:64, 384:388].rearrange("p (h o) -> p h o", o=2)[:, :, 0:1], AF.Exp),
           R=[B0], W=[gcol])
        op('dve', lambda e: e.tensor_copy(lgs.t[:], B2.t[:, 128:384]), R=[B2], W=[lgs])
        yield
        lg = lgs.t[:, 0:128]
        lgC = lgs.t[:, 128:256]
        op('act', lambda e: e.activation(E.t[:, 0, :], lg, AF.Exp), R=[lgs], W=[E])
        op('act', lambda e: e.activation(E.t[:, 1, :], lg, AF.Exp, scale=-1.0), R=[lgs], W=[E])
        op('dve', lambda e: e.tensor_tensor(E.t[:, 2, :], lg, lwd, ALU.subtract), R=[lgs, lw], W=[E])
        op('dve', lambda e: e.tensor_tensor(E.t[:, 3, :], lgC, lg, ALU.subtract), R=[lgs], W=[E])
        yield
        op('act', lambda e: e.activation(E.t[:, 2:4, :], E.t[:, 2:4, :], AF.Exp), R=[E], W=[E])
        ad = aa.t[:, d * 128:(d + 1) * 128]
        kdd = kd.t[:, d * 128:(d + 1) * 128]
        op('pool', lambda e: e.tensor_tensor(akk.t[:], ad, kk.t[:], ALU.mult), R=[aa, kk], W=[akk])
        yield
        op('dve', lambda e: e.tensor_tensor(Q6.t[:, 0, :], r_, E.t[:, 0, :], ALU.mult), R=[ps, E], W=[Q6])
        op('dve', lambda e: e.tensor_tensor(Q6.t[:, 1, :], akk.t[:], E.t[:, 1, :], ALU.mult), R=[akk, E], W=[Q6])
        op('pool', lambda e: e.tensor_tensor(Q6.t[:, 2, :], kdd, E.t[:, 1, :], ALU.mult), R=[kd, E], W=[Q6])
        yield
        op('dve', lambda e: e.scalar_tensor_tensor(Q6.t[:, 3, :], kk.t[:], -1.0, E.t[:, 2, :], ALU.mult, ALU.mult), R=[kk, E], W=[Q6])
        op('pool', lambda e: e.tensor_tensor(Q6.t[:, 4, :], akk.t[:], E.t[:, 3, :], ALU.mult), R=[akk, E], W=[Q6])
        op('pool', lambda e: e.tensor_tensor(Q6.t[:, 5, :], kdd, E.t[:, 3, :], ALU.mult), R=[kd, E], W=[Q6])
        yield
        return

    def back(ln, vis):
        row0, d, is_ctx, chunk, readout, par = vis
        t_ = unpack(ln, vis)
        ps, gg, bsum, Q6, QT, gcol, G, Gk, Xs, Us, yv, Z = (t_[k] for k in ('ps', 'gg', 'bsum', 'Q6', 'QT', 'gcol', 'G', 'Gk', 'Xs', 'Us', 'yv', 'Z'))
        vh = lambda h: ps.t[:, 256 + h * 64:256 + (h + 1) * 64]
        dq = 'sp' if d == 0 else 'pool'
        BA, BB, BC = ln.BA, ln.BB, ln.BC
        src = (3, 0, 1, 2)
        tb = (BA, BB)
        for h in range(2):
            for i in range(4):
                op('pe', lambda e: e.transpose(tb[h].t[0:64, i * 128:(i + 1) * 128], Q6.t[:, src[i], h * 64:(h + 1) * 64], identf.t[:]),
                   R=[Q6, identf], W=[tb[h]], acc=(i > 0))
        yield
        op('act', lambda e: e.copy(QT.t[:, 0, :, :].rearrange("p a t -> p (a t)"), BA.t[0:64, :]), R=[BA], W=[QT])
        op('dve', lambda e: e.tensor_copy(QT.t[:, 1, :, :].rearrange("p a t -> p (a t)"), BB.t[0:64, :]), R=[BB], W=[QT])
        yield
        gb = (BA, BB)
        for h in range(2):
            AR = QT.t[:, h, 0:2, :].rearrange("p a t -> p (a t)")
            BK = QT.t[:, h, 2:4, :].rearrange("p a t -> p (a t)")
            op('pe', lambda e: e.matmul(gb[h].t[:, 0:256], QT.t[:, h, 2, :], AR, start=True, stop=True), R=[QT], W=[gb[h]])
            op('pe', lambda e: e.matmul(gb[h].t[:, 256:512], QT.t[:, h, 3, :], AR, start=True, stop=True), R=[QT], W=[gb[h]], acc=True)
            op('pe', lambda e: e.matmul(BC.t[:, h * 256:(h + 1) * 256], QT.t[:, h, 0, :], BK, start=True, stop=True), R=[QT], W=[BC], acc=(h > 0))
        yield
        C = ln.C[0]
        for h in range(2):
            op('dve', lambda e: e.tensor_tensor(C[h].t[:, 0, :], gb[h].t[:, 0:128], mb[d].t[:, 0, :], ALU.mult), R=[gb[h], mb[d]], W=[C[h]])
            op('dve', lambda e: e.tensor_tensor(C[h].t[:, 3, :], BC.t[:, h * 256:h * 256 + 128], mb[d].t[:, 4, :], ALU.mult), R=[BC, mb[d]], W=[C[h]])
            op('pool', lambda e: e.tensor_copy(C[h].t[:, 1, :], identf.t[:]), R=[identf], W=[C[h]])
            yield
        for h in range(2):
            op('dve', lambda e: e.tensor_tensor(G.t[:, h, :, :].rearrange("p a t -> p (a t)"), gb[h].t[:, 128:512],
                                                mb[d].t[:, 1:4, :].rearrange("p a t -> p (a t)"), ALU.mult), R=[gb[h], mb[d]], W=[G])
        yield
        ib = (BA, BB)
        for lev in range(7):
            Cn = ln.C[(lev + 1) % 2]
            last = (lev == 6)
            for h in range(2):
                op('pe', lambda e: e.matmul(ib[h].t[:, 0:256], C[h].t[:, 3, :], C[h].t[:, 0:2, :].rearrange("p a t -> p (a t)"), start=True, stop=False),
                   R=[C[h]], W=[ib[h]])
                op('pe', lambda e: e.matmul(ib[h].t[:, 128:256], identr.t[:], C[h].t[:, 1, :], start=False, stop=True),
                   R=[C[h], identr], W=[ib[h]], acc=True)
                if not last:
                    op('pe', lambda e: e.matmul(ib[h].t[:, 256:512], C[h].t[:, 0, :], C[h].t[:, 2:4, :].rearrange("p a t -> p (a t)"), start=True, stop=True),
                       R=[C[h]], W=[ib[h]], acc=True)
            yield
            op('act', lambda e: e.copy(Cn[0].t[:].rearrange("p a t -> p (a t)"), ib[0].t[:]), R=[ib[0]], W=[Cn[0]])
            op('dve', lambda e: e.tensor_copy(Cn[1].t[:].rearrange("p a t -> p (a t)"), ib[1].t[:]), R=[ib[1]], W=[Cn[1]])
            yield
            C = Cn
        TTh = [C[h].t[:, 1, :].bitcast(F32) for h in range(2)]
        TTb = [C[0], C[1]]
        for h in range(2):
            hs = slice(h * 64, (h + 1) * 64)
            op('pe', lambda e: e.matmul(BC.t[:, hs], QT.t[:, h, 0, :].bitcast(F32), Z.t[:, hs], start=True, stop=False), R=[QT, Z], W=[BC], acc=(h > 0))
            op('pe', lambda e: e.matmul(BC.t[:, hs], G.t[:, h, 1, :], vh(h), start=False, stop=True), R=[G, ps], W=[BC], acc=True)
        yield
        op('act', lambda e: e.copy(Xs.t[:], BC.t[:, 0:128]), R=[BC], W=[Xs])
        yield
        for h in range(2):
            hs = slice(h * 64, (h + 1) * 64)
            op('pe', lambda e: e.matmul(BC.t[:, 128 + h * 64:128 + (h + 1) * 64], TTh[h], Xs.t[:, hs], start=True, stop=True), R=[TTb[h], Xs], W=[BC], acc=(h > 0))
        yield
        op('dve', lambda e: e.tensor_copy(Us.t[:], BC.t[:, 128:256]), R=[BC], W=[Us])
        yield
        first = True
        if not is_ctx:
            for h in range(2):
                hs = slice(h * 64, (h + 1) * 64)
                ys = slice(256 + h * 64, 256 + (h + 1) * 64)
                op('pe', lambda e: e.matmul(BC.t[:, ys], QT.t[:, h, 1, :].bitcast(F32), Z.t[:, hs], start=True, stop=False), R=[QT, Z], W=[BC], acc=(not first))
                first = False
                op('pe', lambda e: e.matmul(BC.t[:, ys], G.t[:, h, 0, :], Us.t[:, hs], start=False, stop=False), R=[G, Us], W=[BC], acc=True)
                op('pe', lambda e: e.matmul(BC.t[:, ys], G.t[:, h, 2, :], vh(h), start=False, stop=True), R=[G, ps], W=[BC], acc=True)
        for h in range(2):
            hs = slice(h * 64, (h + 1) * 64)
            zs = slice(384 + h * 64, 384 + (h + 1) * 64)
            op('pe', lambda e: e.matmul(BC.t[0:64, zs], Q6.t[:, 4, hs], Us.t[:, hs], start=True, stop=False), R=[Q6, Us], W=[BC], acc=(not first))
            first = False
            op('pe', lambda e: e.matmul(BC.t[0:64, zs], Q6.t[:, 5, hs], vh(h), start=False, stop=True), R=[Q6, ps], W=[BC], acc=True)
        yield
        for h in range(2):
            hs = slice(h * 64, (h + 1) * 64)
            op('dve', lambda e: e.scalar_tensor_tensor(Z.t[:, hs], Z.t[:, hs], gcol.t[:, h, 0:1], BC.t[0:64, 384 + h * 64:384 + (h + 1) * 64], ALU.mult, ALU.add),
               R=[Z, gcol, BC], W=[Z])
        yield
        if is_ctx:
            return
        if not readout:
            op('act', lambda e: e.copy(Yst[d].t[:, chunk, :], BC.t[:, 256:384]), R=[BC], W=[Yst[d]])
            yield
            return
        op('dve', lambda e: e.tensor_tensor(yv.t[:], BC.t[:, 256:384], Yst[1 - d].t[:, chunk, :], ALU.add), R=[BC, Yst[1 - d]], W=[yv])
        yield
        gst, gmv, grs, gnm = ln.gst, ln.gmv, ln.grs, ln.gnm
        for h in range(2):
            op('dve', lambda e: e.bn_stats(gst.t[:, h, :], yv.t[:, h * 64:(h + 1) * 64]), R=[yv], W=[gst])
            op('dve', lambda e: e.bn_aggr(gmv.t[:, h, :], gst.t[:, h, :]), R=[gst], W=[gmv])
        yield
        op('act', lambda e: e.activation(grs.t[:, :, 0:1], gmv.t[:, :, 1:2], AF.Sqrt, bias=64e-5, scale=1.0), R=[gmv], W=[grs])
        yield
        op('dve', lambda e: e.reciprocal(grs.t[:, :, 0:1], grs.t[:, :, 0:1]), R=[grs], W=[grs])
        op('dve', lambda e: e.scalar_tensor_tensor(gnm.t[:, :, 0:1], gmv.t[:, :, 0:1], -1.0, grs.t[:, :, 0:1], ALU.mult, ALU.mult), R=[gmv, grs], W=[gnm])
        yield
        for h in range(2):
            hs = slice(h * 64, (h + 1) * 64)
            op('act', lambda e: e.activation(yv.t[:, hs], yv.t[:, hs], AF.Identity, bias=gnm.t[:, h, 0:1], scale=grs.t[:, h, 0:1]), R=[yv, gnm, grs], W=[yv])
        yield
        op('pool', lambda e: e.tensor_tensor(yv.t[:], yv.t[:], GNG, ALU.mult), R=[yv, rwv], W=[yv])
        op('pool', lambda e: e.tensor_tensor(yv.t[:], yv.t[:], GNB, ALU.add), R=[yv, rwv], W=[yv])
        yield
        for h in range(2):
            hs = slice(h * 64, (h + 1) * 64)
            op('dve', lambda e: e.scalar_tensor_tensor(yv.t[:, hs], vh(h), bsum.t[:, h, 0:1], yv.t[:, hs], ALU.mult, ALU.add),
               R=[ps, bsum, yv], W=[yv])
        y_ = ln.yo[ln.nout % 2]
        ln.nout += 1
        op('dve', lambda e: e.tensor_tensor(y_.t[:], yv.t[:], gg.t[:], ALU.mult), R=[yv, gg], W=[y_])
        kb.dma(dq, ycat.t.ap()[chunk * 128:(chunk + 1) * 128, 128:256], y_.t[:], R=[y_], W=[ycat])
        yield

    def lane_visits(d):
        half = NLAT // 2
        vs = []
        corder = (0, 1) if d == 0 else (1, 0)
        for c in corder:
            vs.append((1 + 128 * c, d, True, c, False))
        lorder = range(NLAT) if d == 0 else range(NLAT - 1, -1, -1)
        for n_ in lorder:
            ro = (n_ >= half) if d == 0 else (n_ < half)
            vs.append((259 + 128 * n_, d, False, n_, ro))
        return [v + (i % 2,) for i, v in enumerate(vs)]

    def interleave(g1, g2):
        a1 = a2 = True
        while a1 or a2:
            if a1:
                try:
                    next(g1)
                except StopIteration:
                    a1 = False
            if a2:
                try:
                    next(g2)
                except StopIteration:
                    a2 = False
            yield

    def lane_gen(d):
        ln = lanes[d]
        op('dve', lambda e: e.memset(ln.Z.t[:], 0.0), W=[ln.Z])
        vs = lane_visits(d)
        yield from front(ln, vs[0])
        for i in range(len(vs)):
            if i + 1 < len(vs):
                yield from interleave(back(ln, vs[i]), front(ln, vs[i + 1]))
            else:
                yield from back(ln, vs[i])

    gens = [lane_gen(0), lane_gen(1)]
    alive = [True, True]
    while any(alive):
        for i in range(2):
            if alive[i]:
                try:
                    next(gens[i])
                except StopIteration:
                    alive[i] = False
    kb.barrier()
    st0.close()
    kb.stack = old


def bcast_mod(kb, dst, modT, j0, identf, onesf):
    with kb.phase():
        dg = [kb.sb("dg%d" % i, [128, 128], F32) for i in range(2)]
        pb_ = [kb.ps("pbc%d" % i, [128, 512], F32) for i in range(2)]
        for k in range(8):
            d_ = dg[k % 2]
            kb.op('dve', lambda e: e.tensor_scalar_mul(d_.t[:], identf.t[:], modT.t[:, j0 + k, 0:1]), R=[identf, modT], W=[d_])
            p = pb_[k % 2]
            kb.op('pe', lambda e: e.matmul(p.t[:, 0:128], onesf.t[:], d_.t[:], start=True, stop=True), R=[onesf, d_], W=[p])
            kb.op('act', lambda e: e.copy(dst.t[:, k * 128:(k + 1) * 128], p.t[:, 0:128]), R=[p], W=[dst])


def stage_oproj(kb, A, ycat, opart, modT, ident, identf, onesf):
    with kb.phase():
        g1t = kb.sb("g1t", [128, D], F32)
        bcast_mod(kb, g1t, modT, 16, identf, onesf)
        wof = kb.sb("wof", [128, 2, D], F32)
        wob = kb.sb("wob", [128, 2, D], BF16)
        kb.dma('sp', wof.t[:], A['wo'].rearrange("(k p) n -> p k n", p=128), W=[wof])
        for k in range(2):
            kb.op('pool', lambda e: e.tensor_tensor(wob.t[:, k, :], wof.t[:, k, :], g1t.t[:], ALU.mult), R=[wof, g1t], W=[wob])
        ysb = [kb.sb("ysb%d" % i, [128, 4, 256], BF16) for i in range(2)]
        yT = [kb.sb("yT%d" % i, [128, 2, 512], BF16) for i in range(2)]
        ptr = [kb.ps("optr%d" % i, [128, 1024], BF16) for i in range(2)]
        po = [kb.ps("opo%d" % i, [128, 512], F32) for i in range(4)]
        ob = [kb.sb("ob%d" % i, [128, D], F32) for i in range(2)]
        cnt = 0
        for blk in range(16):
            ys = ysb[blk % 2]
            yt_ = yT[blk % 2]
            pp = ptr[blk % 2]
            kb.dma('sp', ys.t[:], ycat.t.ap()[blk * 512:(blk + 1) * 512, :].rearrange("(s p) c -> p s c", p=128), R=[ycat], W=[ys])
            for k in range(2):
                for sub in range(4):
                    kb.op('pe', lambda e: e.transpose(pp.t[:, k * 512 + sub * 128: k * 512 + (sub + 1) * 128], ys.t[:, sub, k * 128:(k + 1) * 128], ident.t[:]),
                          R=[ys, ident], W=[pp], acc=(k + sub > 0))
            kb.op('act', lambda e: e.copy(yt_.t[:].rearrange("p k t -> p (k t)"), pp.t[:]), R=[pp], W=[yt_])
            for sub in range(4):
                o = ob[cnt % 2]
                for half in range(2):
                    p = po[(cnt % 2) * 2 + half]
                    for k in range(2):
                        kb.op('pe', lambda e: e.matmul(p.t[:], yt_.t[:, k, sub * 128:(sub + 1) * 128], wob.t[:, k, half * 512:(half + 1) * 512], start=(k == 0), stop=(k == 1)),
                              R=[yt_, wob], W=[p], acc=(k > 0))
                    if half == 0:
                        kb.op('act', lambda e: e.copy(o.t[:, 0:512], p.t[:]), R=[p], W=[o])
                    else:
                        kb.op('dve', lambda e: e.tensor_copy(o.t[:, 512:1024], p.t[:]), R=[p], W=[o])
                tok0 = blk * 512 + sub * 128
                kb.dma('pool', opart.t.ap()[tok0:tok0 + 128, :], o.t[:], R=[o], W=[opart])
                cnt += 1


def stage_b_pre(kb, A, modT, identf, onesf):
    stB = ExitStack()
    old = kb.stack
    kb.stack = stB
    h2T = kb.sb("h2T", [128, 8, 2048], BF16)
    w1b = kb.sb("w1b", [128, 8, 4 * D], BF16)
    w2b = kb.sb("w2b", [128, 32, D], BF16)
    with kb.phase():
        g2t = kb.sb("g2t", [128, D], F32)
        bcast_mod(kb, g2t, modT, 40, identf, onesf)
        ws = [kb.sb("w12st%d" % i, [128, 1024], F32) for i in range(4)]
        n = 0
        for k in range(8):
            for hh in range(4):
                w = ws[n % 4]
                kb.dma('sp', w.t[:], A['w1'][k * 128:(k + 1) * 128, hh * 1024:(hh + 1) * 1024], W=[w])
                eng = ('pool', 'dve', 'act')[n % 3]
                if eng == 'act':
                    kb.op('act', lambda e: e.copy(w1b.t[:, k, hh * 1024:(hh + 1) * 1024], w.t[:]), R=[w], W=[w1b])
                else:
                    kb.op(eng, lambda e: e.tensor_copy(w1b.t[:, k, hh * 1024:(hh + 1) * 1024], w.t[:]), R=[w], W=[w1b])
                n += 1
        for f in range(32):
            w = ws[n % 4]
            kb.dma('sp', w.t[:], A['w2'][f * 128:(f + 1) * 128, :], W=[w])
            kb.op(('pool', 'dve')[n % 2], lambda e: e.tensor_tensor(w2b.t[:, f, :], w.t[:], g2t.t[:], ALU.mult), R=[w, g2t], W=[w2b])
            n += 1
    return dict(stB=stB, old=old, h2T=h2T, w1b=w1b, w2b=w2b)


def stage_b(kb, A, osum, x1d, out, modT, ident, identf, onesf, dbg, nc, pre):
    stB, old, h2T, w1b, w2b = pre['stB'], pre['old'], pre['h2T'], pre['w1b'], pre['w2b']

    with kb.phase():
        lnb = kb.sb("lnb1", [128, 2, D], F32)
        for i in range(2):
            kb.dma('sp', lnb.t[:, i, :], A['lnv'][i:i + 1, :].partition_broadcast(128), W=[lnb])
        ptr = [kb.ps("bptr%d" % i, [128, 1024], BF16) for i in range(2)]
        xt = [kb.sb("bxt%d" % i, [128, D], F32) for i in range(2)]
        ot = [kb.sb("bot%d" % i, [128, D], F32) for i in range(2)]
        xn2 = [kb.sb("xn2_%d" % i, [128, D], BF16) for i in range(2)]
        st_ = [kb.sb("bst%d" % i, [128, 2, 6], F32) for i in range(4)]
        mv_ = [kb.sb("bmv%d" % i, [128, 2], F32) for i in range(4)]
        rstd_ = [kb.sb("brstd%d" % i, [128, 2], F32) for i in range(4)]
        nmr_ = [kb.sb("bnmr%d" % i, [128, 2], F32) for i in range(4)]
        for cnt in range(16):
            tok0 = cnt * 128
            xx = xt[cnt % 2]; oo = ot[cnt % 2]; xp = oo; x1 = xx; xb = xn2[cnt % 2]
            pp = ptr[cnt % 2]
            kb.dma('sp', xx.t[:], A['xb'][tok0:tok0 + 128, :], W=[xx])
            kb.dma('sp', oo.t[:], osum.t.ap()[tok0:tok0 + 128, :], R=[osum], W=[oo])
            kb.op('dve', lambda e: e.scalar_tensor_tensor(xp.t[:], xx.t[:], ALPHA, oo.t[:], ALU.mult, ALU.add), R=[xx, oo], W=[xp])
            st, mv, rstd, nmr = (z[(cnt % 2) * 2] for z in (st_, mv_, rstd_, nmr_))
            ln_stats(kb, xp, xp.t, st, mv, rstd, nmr, 1e-6)
            kb.op('act', lambda e: e.activation(x1.t[:], xp.t[:], AF.Identity, bias=nmr.t[:, 0:1], scale=rstd.t[:, 0:1]), R=[xp, nmr, rstd], W=[x1])
            kb.op('pool', lambda e: e.tensor_tensor(x1.t[:], x1.t[:], lnb.t[:, 0, :], ALU.mult), R=[x1, lnb], W=[x1])
            kb.op('pool', lambda e: e.tensor_tensor(x1.t[:], x1.t[:], lnb.t[:, 1, :], ALU.add), R=[x1, lnb], W=[x1])
            kb.dma('pool', x1d.t.ap()[tok0:tok0 + 128, :], x1.t[:], R=[x1], W=[x1d])
            st, mv, rstd, nmr = (z[(cnt % 2) * 2 + 1] for z in (st_, mv_, rstd_, nmr_))
            ln_stats(kb, x1, x1.t, st, mv, rstd, nmr, 1e-6)
            kb.op('act', lambda e: e.activation(xb.t[:], x1.t[:], AF.Identity, bias=nmr.t[:, 0:1], scale=rstd.t[:, 0:1]), R=[x1, nmr, rstd], W=[xb])
            for k in range(8):
                kb.op('pe', lambda e: e.transpose(pp.t[:, k * 128:(k + 1) * 128], xb.t[:, k * 128:(k + 1) * 128], ident.t[:]),
                      R=[xb, ident], W=[pp], acc=(k > 0))
            for k in range(8):
                kb.op('act', lambda e: e.activation(h2T.t[:, k, tok0:tok0 + 128], pp.t[:, k * 128:(k + 1) * 128],
                                                    AF.Identity, bias=modT.t[:, 24 + k, 0:1], scale=modT.t[:, 32 + k, 0:1]),
                      R=[pp, modT], W=[h2T])
        if 'x1' in dbg:
            for r0 in range(0, 2048, 512):
                kb.dma('sp', dbg['x1'][r0:r0 + 512, :], x1d.t.ap()[r0:r0 + 512, :], R=[x1d], out=True)

    with kb.phase():
        lnb = kb.sb("lnb2", [128, 2, D], F32)
        for i in range(2):
            kb.dma('sp', lnb.t[:, i, :], A['lnv'][2 + i:3 + i, :].partition_broadcast(128), W=[lnb])
        pu = [kb.ps("pu%d" % i, [128, 512], F32) for i in range(2)]
        pacc = [kb.ps("pacc%d" % i, [128, 512], F32) for i in range(4)]
        rl = [kb.sb("rl%d" % i, [128, 256], F32) for i in range(2)]
        h1 = [kb.sb("h1_%d" % i, [128, 256], BF16) for i in range(3)]
        x1 = [kb.sb("cx1_%d" % i, [128, D], F32) for i in range(2)]
        xp = [kb.sb("cxp_%d" % i, [128, D], F32) for i in range(2)]
        ot = [kb.sb("cot_%d" % i, [128, D], F32) for i in range(2)]
        st_ = [kb.sb("cst%d" % i, [128, 2, 6], F32) for i in range(2)]
        mv_ = [kb.sb("cmv%d" % i, [128, 2], F32) for i in range(2)]
        rstd_ = [kb.sb("crstd%d" % i, [128, 2], F32) for i in range(2)]
        nmr_ = [kb.sb("cnmr%d" % i, [128, 2], F32) for i in range(2)]
        cnt = 0
        for blk in range(8):
            t0 = blk * 256
            for f in range(32):
                u = pu[f % 2]
                for k in range(8):
                    kb.op('pe', lambda e: e.matmul(u.t[:, 0:256], w1b.t[:, k, f * 128:(f + 1) * 128], h2T.t[:, k, t0:t0 + 256], start=(k == 0), stop=(k == 7)),
                          R=[w1b, h2T], W=[u], acc=(k > 0))
                rr = rl[f % 2]
                hh = h1[f % 3]
                kb.op('act', lambda e: e.activation(rr.t[:], u.t[:, 0:256], AF.Relu), R=[u], W=[rr])
                kb.op('dve', lambda e: e.tensor_tensor(hh.t[:], rr.t[:], rr.t[:], ALU.mult), R=[rr], W=[hh])
                for sub in range(2):
                    for half in range(2):
                        pa = pacc[sub * 2 + half]
                        kb.op('pe', lambda e: e.matmul(pa.t[:], hh.t[:, sub * 128:(sub + 1) * 128], w2b.t[:, f, half * 512:(half + 1) * 512], start=(f == 0), stop=(f == 31)),
                              R=[hh, w2b], W=[pa], acc=(f > 0))
            for sub in range(2):
                tok0 = t0 + sub * 128
                xx = x1[cnt % 2]; xq = xp[cnt % 2]; oo = ot[cnt % 2]
                cnt += 1
                kb.dma('sp', xx.t[:], x1d.t.ap()[tok0:tok0 + 128, :], R=[x1d], W=[xx])
                for half in range(2):
                    pa = pacc[sub * 2 + half]
                    kb.op('dve', lambda e: e.scalar_tensor_tensor(xq.t[:, half * 512:(half + 1) * 512], xx.t[:, half * 512:(half + 1) * 512], ALPHA, pa.t[:], ALU.mult, ALU.add),
                          R=[xx, pa], W=[xq])
                st, mv, rstd, nmr = (z[cnt % 2] for z in (st_, mv_, rstd_, nmr_))
                ln_stats(kb, xq, xq.t, st, mv, rstd, nmr, 1e-6)
                kb.op('act', lambda e: e.activation(oo.t[:], xq.t[:], AF.Identity, bias=nmr.t[:, 0:1], scale=rstd.t[:, 0:1]), R=[xq, nmr, rstd], W=[oo])
                kb.op('pool', lambda e: e.tensor_tensor(oo.t[:], oo.t[:], lnb.t[:, 0, :], ALU.mult), R=[oo, lnb], W=[oo])
                kb.op('pool', lambda e: e.tensor_tensor(oo.t[:], oo.t[:], lnb.t[:, 1, :], ALU.add), R=[oo, lnb], W=[oo])
                kb.dma('sp', out[tok0:tok0 + 128, :], oo.t[:], R=[oo], out=True)
    kb.barrier()
    stB.close()
    kb.stack = old


def _const_tables():
    t = np.arange(T)
    row = (t // 64).astype(np.float32)
    col = (t % 64).astype(np.float32)
    inv = (np.float32(10000.0) ** (-np.arange(16, dtype=np.float32) / np.float32(16))).astype(np.float32)
    cosd = np.zeros((128, T), np.float32)
    sind = np.zeros((128, T), np.float32)
    for p in range(128):
        d = p % 64
        pos = row if d < 32 else col
        i = d % 16
        ang = (pos * inv[i]).astype(np.float32)
        cosd[p] = np.cos(ang)
        sind[p] = np.sin(ang) * (-1.0 if (d % 32) < 16 else 1.0)
    return cosd, sind


def _bias_tables(rpb, heads):
    kk = np.arange(128)[:, None]
    qq = np.arange(128)[None, :]
    biasg = np.zeros((128, 5, 2, 640), np.float32)
    maskc = np.zeros((128, 5, 2, 640), np.float32)
    for v, g in enumerate((0, 1, 30, 62, 63)):
        m0 = min(max(g - 2, 0), 59)
        i = 2 * g + qq // 64
        cq = qq % 64
        r0 = np.clip(i - 4, 0, 120)
        c0 = np.clip(cq - 8, 0, 48)
        for j in range(5):
            kr = 2 * (m0 + j) + kk // 64
            kc = kk % 64
            valid = (kr >= r0) & (kr < r0 + 8) & (kc >= c0) & (kc < c0 + 16)
            ro = np.clip(kr - i + 7, 0, 14)
            co = np.clip(kc - cq + 15, 0, 30)
            for hd in range(2):
                biasg[:, v, hd, j * 128:(j + 1) * 128] = rpb[heads[hd]][ro, co]
                maskc[:, v, hd, j * 128:(j + 1) * 128] = np.where(valid, 0.0, NEG)
    return biasg.reshape(128, -1), maskc.reshape(128, -1)


def _swap_idx():
    d = np.arange(64)
    return np.where((d % 32) < 16, d + 16, d - 16)


def _prep(inputs):
    f = lambda k: np.asarray(inputs[k], dtype=np.float32)
    x, c, ctx, c_ctx = f('x'), f('c'), f('ctx'), f('c_ctx')
    w_in = f('w_in')[0]
    w_out = f('w_out')[0]
    rpb = f('na_rpb')[0]
    cosd, sind = _const_tables()
    sw = _swap_idx()
    maps = []
    for core in range(8):
        b, hg = core // 4, core % 4
        heads = (2 * hg, 2 * hg + 1)
        hc = np.concatenate([h * 64 + np.arange(64) for h in heads])
        hcs = np.concatenate([h * 64 + sw for h in heads])
        RW = 1536
        cols = np.concatenate([hc, hcs, 512 + hc, 512 + hcs, 1024 + hc,
                               RW + hc, RW + 512 + hc, RW + 1024 + hc,
                               RW + 1536 + np.arange(128), RW + 1664 + np.arange(128), RW + 1792 + np.arange(128)])
        biasg, maskc = _bias_tables(rpb, heads)
        m = {
            'xa': np.ascontiguousarray(np.concatenate([ctx[b], x[b]], 0)),
            'cct': np.ascontiguousarray(np.stack([c[b], c_ctx], 1)),
            'wmod': f('w_mod')[0],
            'bmod': np.ascontiguousarray(np.stack([f('b_mod')[0]] * 2, 1)),
            'wina': np.ascontiguousarray(w_in[:, cols]),
            'cosd': cosd, 'sind': sind, 'biasg': biasg, 'maskc': maskc,
            'wo': np.ascontiguousarray(np.concatenate([w_out[hg * 128:(hg + 1) * 128], w_out[512 + hg * 128:512 + (hg + 1) * 128]], 0)),
            'xb': np.ascontiguousarray(x[b, hg * 2048:(hg + 1) * 2048]),
            'w1': f('mlp_w1')[0], 'w2': f('mlp_w2')[0],
            'lnv': np.ascontiguousarray(np.stack([f('ln1_g')[0], f('ln1_b')[0], f('ln2_g')[0], f('ln2_b')[0]], 0)),
        }
        m.update(_prep_rw(inputs, b, hg))
        maps.append(m)
    return maps


def _prep_rw(inputs, b, hg):
    f = lambda k: np.asarray(inputs[k], dtype=np.float32)[0]
    heads = (2 * hg, 2 * hg + 1)
    hc = np.concatenate([h * 64 + np.arange(64) for h in heads])
    cols = np.concatenate([hc, 512 + hc, 1024 + hc, 1536 + np.arange(128), 1664 + np.arange(128), 1792 + np.arange(128)])
    rwv = np.zeros((4, 768), np.float32)
    rwv[0] = f('rw_mu_prev')[cols]
    rwv[1] = f('rw_mu_next')[cols]
    w0, a0 = f('rw_w0'), f('rw_a0')
    rwv[2, 0:128] = w0[0][hc]; rwv[2, 128:256] = w0[1][hc]
    rwv[2, 256:384] = a0[0][hc]; rwv[2, 384:512] = a0[1][hc]
    rwv[2, 512:640] = f('rw_k_k')[hc]
    rwv[3, 0:128] = f('rw_r_k').reshape(-1)[hc]
    rwv[3, 128:256] = f('rw_gn_g')[hc]
    rwv[3, 256:384] = f('rw_gn_b')[hc]
    rwv[3, 384:512] = f('rw_k_a')[hc]
    rwv[3, 512:640] = f('rw_k_a')[hc]
    rwv16 = np.zeros((16, 768), np.float32)
    rwv16[0:4] = rwv
    w2 = f('rw_w2')
    a2 = f('rw_a2')
    rww2 = np.zeros((128, 256), np.float32)
    rwa2 = np.zeros((128, 256), np.float32)
    for dd in range(2):
        rww2[dd * 64:(dd + 1) * 64, dd * 128:(dd + 1) * 128] = w2[dd][:, hc]
        rwa2[dd * 64:(dd + 1) * 64, dd * 128:(dd + 1) * 128] = a2[dd][:, hc]
    rwg2 = np.ascontiguousarray(f('rw_g2')[:, hc])
    i = np.arange(128)[:, None]
    j = np.arange(128)[None, :]
    tri = np.zeros((128, 768), np.float32)
    tri[:, 0:128] = (i < j)
    tri[:, 128:256] = (i > j)
    tri[:, 256:384] = (i <= j)
    tri[:, 384:512] = (i >= j)
    return {'rwv': rwv16, 'rww2': rww2, 'rwa2': rwa2, 'rwg2': rwg2, 'tri': tri}


_NC = None


def kernel(**inputs):
    global _NC
    if _NC is None:
        _NC = build_program()
    maps = _prep(inputs)
    res = run_bass_kernel_spmd(_NC, maps, core_ids=list(range(8)))
    out = np.zeros((2, T, D), np.float32)
    for core in range(8):
        b, q = core // 4, core % 4
        out[b, q * 2048:(q + 1) * 2048] = np.asarray(res.results[core]['out'])
    return out
```
